# Optimizing a Trainium2 kernel written in Bass

```python
import jax, jax.numpy as jnp
from jax import lax
import numpy as np

D_MODEL = 2048
BATCH = 4
SEQ = 2048
DEPTH = 1
DEC_BATCH = 128
DEC_SEQ = 8
PAST_LEN = 16384
PAGE_SIZE = 128

MIX_WIDTH = D_MODEL
GDN_HEADS = 8
GDN_DK = 128
GDN_DV = 128
RET_HEADS = 4
RET_DK = 256
RET_DV = 256
CONV_WIDTH = 4
CHUNK = 64
N_MEM = 256
MEM_HEADS = 4
MEM_HEAD_DIM = D_MODEL // MEM_HEADS
D_FF = -(-8 * D_MODEL // (3 * 256)) * 256
EPS = 1e-6
ROPE_BASE = 10000.0

GDN_QK = GDN_HEADS * GDN_DK
GDN_V = GDN_HEADS * GDN_DV
GDN_CONV_DIM = 2 * GDN_QK + GDN_V
RET_QK = RET_HEADS * RET_DK
RET_V = RET_HEADS * RET_DV
IN_WIDTH = GDN_CONV_DIM + GDN_V + 2 * GDN_HEADS + 2 * RET_QK + 2 * RET_V

kernel_name = "hybrid_gdn_retention_memory_step"


def rms_norm(x, g):
    xf = x.astype(jnp.float32)
    y = xf * lax.rsqrt(jnp.mean(xf * xf, -1, keepdims=True) + EPS)
    return (y * g.astype(jnp.float32)).astype(x.dtype)


def head_rms(x, g):
    return x * lax.rsqrt(jnp.mean(x * x, -1, keepdims=True) + EPS) * g.astype(jnp.float32)


def l2norm(x):
    return x * lax.rsqrt(jnp.sum(x * x, -1, keepdims=True) + EPS)


def short_conv(u, buf, w):
    L = u.shape[1]
    full = jnp.concatenate([buf, u], axis=1)
    out = full[:, 0:L] * w[:, 0]
    for i in range(1, CONV_WIDTH):
        out = out + full[:, i:i + L] * w[:, i]
    return jax.nn.silu(out), full[:, full.shape[1] - (CONV_WIDTH - 1):]


def rotary(x, pos):
    half = x.shape[-1] // 2
    inv = ROPE_BASE ** (-jnp.linspace(0.0, 1.0, half, dtype=jnp.float32))
    ang = pos[:, None] * inv[None, :]
    cos = jnp.cos(ang)[None, :, None, :]
    sin = jnp.sin(ang)[None, :, None, :]
    x2 = x.reshape(x.shape[:-1] + (half, 2))
    xe, xo = x2[..., 0], x2[..., 1]
    return jnp.stack([xe * cos - xo * sin, xo * cos + xe * sin], -1).reshape(x.shape)


def _pad_len(t, pad):
    return jnp.pad(t, [(0, 0), (0, pad)] + [(0, 0)] * (t.ndim - 2))


def _chunk(t, C):
    B, Lp = t.shape[:2]
    t = t.reshape((B, Lp // C, C) + t.shape[2:])
    return jnp.moveaxis(t, 3, 1)


def _unchunk(o, L):
    B, H, NC, C, D = o.shape
    return jnp.moveaxis(o.reshape(B, H, NC * C, D), 1, 2)[:, :L]


def gated_delta_chunked(q, k, v, g, beta, s0):
    L = q.shape[1]
    C = min(CHUNK, L)
    pad = (-L) % C
    q, k, v, g, beta = [_chunk(_pad_len(t, pad), C) for t in (q, k, v, g, beta)]
    G = jnp.cumsum(g, -1)
    causal = jnp.tril(jnp.ones((C, C), bool))
    strict = jnp.tril(jnp.ones((C, C), bool), -1)
    decay = jnp.exp(jnp.where(causal, G[..., :, None] - G[..., None, :], -jnp.inf))
    kk = jnp.einsum('bhncd,bhnmd->bhncm', k, k)
    lower = jnp.where(strict, beta[..., :, None] * kk * decay, 0.0)
    a_mat = lower + jnp.eye(C, dtype=lower.dtype)
    rhs = jnp.concatenate([v * beta[..., None], k * (beta * jnp.exp(G))[..., None]], -1)
    sol = lax.linalg.triangular_solve(a_mat, rhs, left_side=True, lower=True, unit_diagonal=True)
    u_part, w_part = sol[..., :GDN_DV], sol[..., GDN_DV:]
    qk = jnp.einsum('bhncd,bhnmd->bhncm', q, k) * decay
    q_dec = q * jnp.exp(G)[..., None]
    k_dec = k * jnp.exp(G[..., -1:] - G)[..., None]
    g_last = jnp.exp(G[..., -1])

    def step(S, xs):
        u_c, w_c, qk_c, qd_c, kd_c, gl_c = xs
        u = u_c - jnp.einsum('bhck,bhkv->bhcv', w_c, S)
        o = jnp.einsum('bhck,bhkv->bhcv', qd_c, S) + jnp.einsum('bhcm,bhmv->bhcv', qk_c, u)
        S = S * gl_c[..., None, None] + jnp.einsum('bhck,bhcv->bhkv', kd_c, u)
        return S, o

    xs = tuple(jnp.moveaxis(t, 2, 0) for t in (u_part, w_part, qk, q_dec, k_dec, g_last))
    S, o = lax.scan(step, s0, xs)
    return _unchunk(jnp.moveaxis(o, 0, 2), L), S


def retention_chunked(q, k, v, log_gamma, s0):
    B, L, H, _ = q.shape
    C = min(CHUNK, L)
    pad = (-L) % C
    lg = jnp.broadcast_to(log_gamma, (B, L, H))
    q, k, v, lg = [_chunk(_pad_len(t, pad), C) for t in (q, k, v, lg)]
    G = jnp.cumsum(lg, -1)
    causal = jnp.tril(jnp.ones((C, C), bool))
    decay = jnp.exp(jnp.where(causal, G[..., :, None] - G[..., None, :], -jnp.inf))
    intra = jnp.einsum('bhncm,bhnmv->bhncv', jnp.einsum('bhncd,bhnmd->bhncm', q, k) * decay, v)
    q_dec = q * jnp.exp(G)[..., None]
    k_dec = k * jnp.exp(G[..., -1:] - G)[..., None]
    g_last = jnp.exp(G[..., -1])

    def step(S, xs):
        in_c, qd_c, kd_c, v_c, gl_c = xs
        o = in_c + jnp.einsum('bhck,bhkv->bhcv', qd_c, S)
        S = S * gl_c[..., None, None] + jnp.einsum('bhck,bhcv->bhkv', kd_c, v_c)
        return S, o

    xs = tuple(jnp.moveaxis(t, 2, 0) for t in (intra, q_dec, k_dec, v, g_last))
    S, o = lax.scan(step, s0, xs)
    return _unchunk(jnp.moveaxis(o, 0, 2), L), S


def hybrid_mixer(h, pos0, conv_buf, s_gdn, s_ret, w_in, conv_w, a_log, dt_bias, gdn_norm, ret_norm, w_out):
    B, L, _ = h.shape
    f32 = jnp.float32
    proj = jnp.einsum('bld,de->ble', h, w_in).astype(f32)
    i0 = GDN_CONV_DIM
    i1 = i0 + GDN_V
    i2 = i1 + GDN_HEADS
    i3 = i2 + GDN_HEADS
    i4 = i3 + RET_QK
    i5 = i4 + RET_QK
    i6 = i5 + RET_V
    conv_in, z, b, a, rq, rk, rv, rg = jnp.split(proj, [i0, i1, i2, i3, i4, i5, i6], axis=-1)
    conv_out, new_buf = short_conv(conv_in, conv_buf.astype(f32), conv_w.astype(f32))
    gq, gk, gv = jnp.split(conv_out, [GDN_QK, 2 * GDN_QK], axis=-1)
    gq = l2norm(gq.reshape(B, L, GDN_HEADS, GDN_DK)) * (GDN_DK ** -0.5)
    gk = l2norm(gk.reshape(B, L, GDN_HEADS, GDN_DK))
    gv = gv.reshape(B, L, GDN_HEADS, GDN_DV)
    beta = jax.nn.sigmoid(b)
    g = -jnp.exp(a_log.astype(f32)) * jax.nn.softplus(a + dt_bias.astype(f32))
    o_g, s_gdn_new = gated_delta_chunked(gq, gk, gv, g, beta, s_gdn.astype(f32))
    o_g = head_rms(o_g, gdn_norm) * jax.nn.silu(z.reshape(B, L, GDN_HEADS, GDN_DV))
    pos = jnp.arange(L, dtype=f32) + pos0
    rq = rotary(rq.reshape(B, L, RET_HEADS, RET_DK), pos)
    rk = rotary(rk.reshape(B, L, RET_HEADS, RET_DK), pos) * (RET_DK ** -0.5)
    rv = rv.reshape(B, L, RET_HEADS, RET_DV)
    log_gamma = jnp.log1p(-jnp.exp2(-5.0 - jnp.arange(RET_HEADS, dtype=f32)))
    o_r, s_ret_new = retention_chunked(rq, rk, rv, log_gamma, s_ret.astype(f32))
    o_r = head_rms(o_r, ret_norm.reshape(RET_HEADS, RET_DV)) * jax.nn.silu(rg.reshape(B, L, RET_HEADS, RET_DV))
    mix = jnp.concatenate([o_g.reshape(B, L, GDN_V), o_r.reshape(B, L, RET_V)], -1).astype(h.dtype)
    return jnp.einsum('ble,ed->bld', mix, w_out), new_buf, s_gdn_new, s_ret_new


def memory_kv(mem, norm_mem_in, w_mem_k, w_mem_v):
    B = mem.shape[0]
    m = rms_norm(mem, norm_mem_in)
    k = jnp.einsum('bmd,de->bme', m, w_mem_k).reshape(B, N_MEM, MEM_HEADS, MEM_HEAD_DIM)
    v = jnp.einsum('bmd,de->bme', m, w_mem_v).reshape(B, N_MEM, MEM_HEADS, MEM_HEAD_DIM)
    return k, v


def memory_attention(h, mem_k, mem_v, w_mem_q, w_mem_o):
    B, L, _ = h.shape
    q = jnp.einsum('bld,de->ble', h, w_mem_q).reshape(B, L, MEM_HEADS, MEM_HEAD_DIM)
    s = jnp.einsum('blhd,bmhd->bhlm', q, mem_k.astype(q.dtype)).astype(jnp.float32) * (MEM_HEAD_DIM ** -0.5)
    p = jax.nn.softmax(s, axis=-1).astype(q.dtype)
    o = jnp.einsum('bhlm,bmhd->blhd', p, mem_v.astype(q.dtype)).reshape(B, L, MEM_HEADS * MEM_HEAD_DIM)
    return jnp.einsum('ble,ed->bld', o, w_mem_o)


def swiglu(h, w_gate, w_up, w_down):
    a = jnp.einsum('bld,df->blf', h, w_gate)
    u = jnp.einsum('bld,df->blf', h, w_up)
    return jnp.einsum('blf,fd->bld', jax.nn.silu(a) * u, w_down)


def layer(x, pos0, conv_buf, s_gdn, s_ret, mem_k, mem_v,
          norm_mix, w_in, conv_w, gdn_a_log, gdn_dt_bias, gdn_norm, ret_norm, w_out,
          norm_cross, w_mem_q, w_mem_o, norm_ffn, w_gate, w_up, w_down):
    mix, conv_buf, s_gdn, s_ret = hybrid_mixer(rms_norm(x, norm_mix), pos0, conv_buf, s_gdn, s_ret,
                                               w_in, conv_w, gdn_a_log, gdn_dt_bias, gdn_norm, ret_norm, w_out)
    x = x + mix
    x = x + memory_attention(rms_norm(x, norm_cross), mem_k, mem_v, w_mem_q, w_mem_o)
    x = x + swiglu(rms_norm(x, norm_ffn), w_gate, w_up, w_down)
    return x, conv_buf, s_gdn, s_ret


def setup_inputs(seed: int = 0) -> dict:
    key = jax.random.key(seed)
    ks = jax.random.split(key, 32)
    f32 = jnp.float32
    nrm = lambda k, shape, s: jax.random.normal(k, shape, f32) * s
    gain = lambda k, shape: 1.0 + 0.02 * jax.random.normal(k, shape, f32)
    L = DEPTH
    return {
        "x_prompt": nrm(ks[0], (BATCH, SEQ, D_MODEL), 1.0),
        "x_sample": nrm(ks[1], (DEC_BATCH, DEC_SEQ, D_MODEL), 1.0),
        "mem_prompt": nrm(ks[2], (BATCH, N_MEM, D_MODEL), 1.0),
        "cache_mem_k": nrm(ks[3], (L, DEC_BATCH, N_MEM, MEM_HEADS, MEM_HEAD_DIM), 1.0),
        "cache_mem_v": nrm(ks[4], (L, DEC_BATCH, N_MEM, MEM_HEADS, MEM_HEAD_DIM), 1.0),
        "state_gdn": nrm(ks[5], (L, DEC_BATCH, GDN_HEADS, GDN_DK, GDN_DV), 0.1),
        "state_gdn_conv": nrm(ks[6], (L, DEC_BATCH, CONV_WIDTH - 1, GDN_CONV_DIM), 1.0),
        "state_ret": nrm(ks[7], (L, DEC_BATCH, RET_HEADS, RET_DK, RET_DV), 0.1),
        "norm_mix": gain(ks[8], (L, D_MODEL)),
        "w_in": nrm(ks[9], (L, D_MODEL, IN_WIDTH), D_MODEL ** -0.5),
        "conv_w": nrm(ks[10], (L, GDN_CONV_DIM, CONV_WIDTH), CONV_WIDTH ** -0.5),
        "gdn_a_log": jnp.log(jax.random.uniform(ks[11], (L, GDN_HEADS), f32, 1.0, 16.0)),
        "gdn_dt_bias": nrm(ks[12], (L, GDN_HEADS), 0.1),
        "gdn_norm": gain(ks[13], (L, GDN_DV)),
        "ret_norm": gain(ks[14], (L, RET_V)),
        "w_out": nrm(ks[15], (L, MIX_WIDTH, D_MODEL), MIX_WIDTH ** -0.5),
        "norm_mem_in": gain(ks[16], (L, D_MODEL)),
        "norm_cross": gain(ks[17], (L, D_MODEL)),
        "w_mem_q": nrm(ks[18], (L, D_MODEL, MEM_HEADS * MEM_HEAD_DIM), D_MODEL ** -0.5),
        "w_mem_k": nrm(ks[19], (L, D_MODEL, MEM_HEADS * MEM_HEAD_DIM), D_MODEL ** -0.5),
        "w_mem_v": nrm(ks[20], (L, D_MODEL, MEM_HEADS * MEM_HEAD_DIM), D_MODEL ** -0.5),
        "w_mem_o": nrm(ks[21], (L, MEM_HEADS * MEM_HEAD_DIM, D_MODEL), (MEM_HEADS * MEM_HEAD_DIM) ** -0.5),
        "norm_ffn": gain(ks[22], (L, D_MODEL)),
        "w_gate": nrm(ks[23], (L, D_MODEL, D_FF), D_MODEL ** -0.5),
        "w_up": nrm(ks[24], (L, D_MODEL, D_FF), D_MODEL ** -0.5),
        "w_down": nrm(ks[25], (L, D_FF, D_MODEL), D_FF ** -0.5),
        "norm_final": gain(ks[26], (D_MODEL,)),
    }


def reference(x_prompt, x_sample, mem_prompt, cache_mem_k, cache_mem_v, state_gdn, state_gdn_conv, state_ret,
              norm_mix, w_in, conv_w, gdn_a_log, gdn_dt_bias, gdn_norm, ret_norm, w_out,
              norm_mem_in, norm_cross, w_mem_q, w_mem_k, w_mem_v, w_mem_o,
              norm_ffn, w_gate, w_up, w_down, norm_final):
    f32 = jnp.float32

    def weights(l):
        return (norm_mix[l], w_in[l], conv_w[l], gdn_a_log[l], gdn_dt_bias[l], gdn_norm[l], ret_norm[l], w_out[l],
                norm_cross[l], w_mem_q[l], w_mem_o[l], norm_ffn[l], w_gate[l], w_up[l], w_down[l])

    pdt = x_prompt.dtype
    y = x_prompt
    p_gdn, p_conv, p_ret, p_mk, p_mv = [], [], [], [], []
    for l in range(DEPTH):
        mk, mv = memory_kv(mem_prompt, norm_mem_in[l], w_mem_k[l], w_mem_v[l])
        conv0 = jnp.zeros((BATCH, CONV_WIDTH - 1, GDN_CONV_DIM), f32)
        sg0 = jnp.zeros((BATCH, GDN_HEADS, GDN_DK, GDN_DV), f32)
        sr0 = jnp.zeros((BATCH, RET_HEADS, RET_DK, RET_DV), f32)
        y, cb, sg, sr = layer(y, 0, conv0, sg0, sr0, mk, mv, *weights(l))
        p_gdn.append(sg.astype(pdt))
        p_conv.append(cb.astype(pdt))
        p_ret.append(sr.astype(pdt))
        p_mk.append(mk)
        p_mv.append(mv)
    y_prompt = rms_norm(y, norm_final)

    sdt = x_sample.dtype
    ys = x_sample
    s_gdn, s_conv, s_ret = [], [], []
    for l in range(DEPTH):
        ys, cb, sg, sr = layer(ys, PAST_LEN, state_gdn_conv[l], state_gdn[l], state_ret[l],
                               cache_mem_k[l], cache_mem_v[l], *weights(l))
        s_gdn.append(sg.astype(sdt))
        s_conv.append(cb.astype(sdt))
        s_ret.append(sr.astype(sdt))
    y_sample = rms_norm(ys, norm_final)

    return (y_prompt, y_sample,
            jnp.stack(p_gdn), jnp.stack(p_conv), jnp.stack(p_ret), jnp.stack(p_mk), jnp.stack(p_mv),
            jnp.stack(s_gdn), jnp.stack(s_conv), jnp.stack(s_ret))
```

```python
import numpy as np
from contextlib import ExitStack
import concourse.bass as bass
import concourse.mybir as mybir
from concourse.bass_utils import run_bass_kernel_spmd

F32 = mybir.dt.float32
BF16 = mybir.dt.bfloat16
AF = mybir.ActivationFunctionType
ALU = mybir.AluOpType

D = 2048
NPRE = 8
NFULL = 8
NT = 17
TF = 1152
DFF = 5632
EPS = 1e-6
NEG = -30000.0
PCHUNK = 128
import os
ATTACH_WAIT = os.environ.get('ATTACH_WAIT', '1') == '1'


class _Stop(Exception):
    pass


class Sched:
    ENGS = ("pe", "act", "dve", "pool", "sp")

    def __init__(self, nc):
        self.nc = nc
        self.ops = {e: [] for e in self.ENGS}
        self.count = {e: 0 for e in self.ENGS}
        self.seen = {e: {} for e in self.ENGS}
        self.keys = {}
        self.dcount = {}
        self.sem_handles = {}
        self.nrec = 0
        self.limit = None
        self.bank_of = {}
        self.capture = None
        self.cap_sfx = ""
        self.cap_shared = set()
        self.out_toks = []

    def _ck(self, keys):
        out = []
        for k_ in keys:
            if k_ in self.cap_shared or (isinstance(k_, tuple) and k_[0] in self.cap_shared):
                out.append(k_)
            elif isinstance(k_, tuple):
                out.append((k_[0] + self.cap_sfx,) + tuple(k_[1:]))
            else:
                out.append(k_ + self.cap_sfx)
        return out

    def replay(self, lists):
        n = max(len(l) for l in lists)
        for i in range(n):
            for l in lists:
                if i < len(l):
                    it = l[i]
                    if it[0] == "op":
                        self.op(*it[1:])
                    else:
                        _, q, out, in_, sem, reads, writes, final, kw = it
                        tok = self.dma(q, out, in_, sem, reads, writes, **kw)
                        if final:
                            self.out_toks.append(tok)

    def dma_out(self, q, out, in_, sem, reads=(), writes=(), **kw):
        if self.capture is not None:
            self.capture.append(("dma", q, out, in_, sem + self.cap_sfx, self._ck(reads), self._ck(writes), True, kw))
            return
        self.out_toks.append(self.dma(q, out, in_, sem, reads, writes, **kw))

    def _tick(self):
        self.nrec += 1
        if self.limit and self.nrec > self.limit:
            raise _Stop()

    def _deps(self, eng, reads, writes):
        toks = {}

        def add(tok, raw):
            if tok is None:
                return
            s, v = tok
            if s == eng and not raw:
                return
            if self.seen[eng].get(s, 0) >= v:
                return
            if toks.get(s, 0) < v:
                toks[s] = v

        for k in reads:
            st = self.keys.get(k)
            if st:
                add(st[0], True)
        for k in writes:
            st = self.keys.get(k)
            if st:
                add(st[0], False)
                for r in st[1]:
                    add(r, False)
        for s, v in toks.items():
            if s in self.count:
                assert v <= self.count[s], f"{eng} waits on pending inc of {s}: {v}>{self.count[s]}"
            self.seen[eng][s] = v
        return list(toks.items())

    def _mark(self, tok, reads, writes):
        for k in reads:
            st = self.keys.setdefault(k, [None, []])
            st[1].append(tok)
        for k in writes:
            self.keys[k] = [tok, []]

    def op(self, eng, fn, reads=(), writes=(), inc=True):
        if self.capture is not None:
            self.capture.append(("op", eng, fn, self._ck(reads), self._ck(writes), inc))
            return None
        self._tick()
        banks = {self.bank_of[k] for k in list(reads) + list(writes) if k in self.bank_of}
        if banks:
            writes = list(writes) + sorted(banks)
        waits = self._deps(eng, reads, writes)
        if inc:
            self.count[eng] += 1
            tok = (eng, self.count[eng])
        else:
            tok = (eng, self.count[eng] + 1)
        self.ops[eng].append((fn, waits, eng if inc else None, 1))
        self._mark(tok, reads, writes)
        return tok

    def dma(self, q, out, in_, sem, reads=(), writes=(), **kw):
        if self.capture is not None:
            self.capture.append(("dma", q, out, in_, sem + self.cap_sfx, self._ck(reads), self._ck(writes), False, kw))
            return None
        self._tick()
        waits = self._deps(q, reads, writes)
        prev = self.dcount.get(sem, 0)
        if prev > self.seen[q].get(sem, 0):
            self.seen[q][sem] = prev
            waits = [w for w in waits if w[0] != sem] + [(sem, prev)]
        self.dcount[sem] = self.dcount.get(sem, 0) + 16
        tok = (sem, self.dcount[sem])
        self.ops[q].append((lambda e: e.dma_start(out=out, in_=in_, **kw), waits, sem, 16))
        self._mark(tok, reads, writes)
        return tok

    def wait_all(self, eng, toks):
        waits = []
        for s, v in toks:
            if self.seen[eng].get(s, 0) < v:
                self.seen[eng][s] = v
                waits.append((s, v))
        self.ops[eng].append((None, waits, None, 0))

    def barrier(self):
        toks = [(e, self.count[e]) for e in self.ENGS if self.count[e] > 0]
        toks += [(s, v) for s, v in self.dcount.items()]
        for e in self.ENGS:
            self.wait_all(e, [t for t in toks if t[0] != e])
        self.keys = {}

    def emit(self):
        nc = self.nc
        names = list(self.ENGS) + sorted(self.dcount.keys())
        with ExitStack() as es:
            for n in names:
                self.sem_handles[n] = es.enter_context(nc.semaphore("s_" + n))
            block = es.enter_context(nc.Block())
            H = self.sem_handles

            def mk(e):
                def body(engine):
                    for fn, waits, incsem, incv in self.ops[e]:
                        if fn is None or not ATTACH_WAIT:
                            for s, v in waits:
                                engine.wait_ge(H[s], v)
                            if fn is None:
                                continue
                            ins = fn(engine)
                        else:
                            for s, v in waits[:-1]:
                                engine.wait_ge(H[s], v)
                            ins = fn(engine)
                            if waits:
                                ins._wait_ge(H[waits[-1][0]], waits[-1][1])
                        if incsem is not None:
                            ins.then_inc(H[incsem], incv)
                return body

            block.tensor(mk("pe"))
            block.scalar(mk("act"))
            block.vector(mk("dve"))
            block.gpsimd(mk("pool"))
            block.sync(mk("sp"))


class Arena:
    def __init__(self, t, n):
        self.t, self.n, self.off = t, n, 0

    def reset(self):
        self.off = 0

    def alloc(self, shape, dt=F32):
        n = int(np.prod(shape[1:]))
        nf = n if dt == F32 else (n + 1) // 2
        assert self.off + nf <= self.n, f"arena overflow {self.off}+{nf}>{self.n}"
        ap = self.t[:, self.off:self.off + nf]
        self.off += nf
        if dt != F32:
            ap = ap.bitcast(dt)[:, 0:n]
        if len(shape) == 3:
            ap = ap.rearrange("p (a b) -> p a b", a=shape[1])
        elif len(shape) == 4:
            ap = ap.rearrange("p (a b c) -> p a b c", a=shape[1], b=shape[2])
        return ap


def _masks(C):
    t = np.arange(128)
    g = t // C
    same = g[:, None] == g[None, :]
    cum = (same & (t[:, None] <= t[None, :])).astype(np.float32)
    grp = same.astype(np.float32)
    negS = np.where(same & (t[:, None] > t[None, :]), 0.0, NEG).astype(np.float32)
    negC = np.where(same & (t[None, :] >= t[:, None]), 0.0, NEG).astype(np.float32)
    G = 128 // C
    row = (g[:, None] == np.arange(G)[None, :]).astype(np.float32)
    return cum, grp, negS, negC, row


def _ret_consts(C):
    t = np.arange(128)
    j = (t % C).astype(np.float32)
    out = []
    for hr in range(4):
        lg = np.log1p(-np.exp2(np.float32(-5.0 - hr))).astype(np.float32)
        Gt = lg * (j + 1.0)
        GL = lg * C
        kd = (np.exp(GL - Gt) * (256.0 ** -0.5)).astype(np.float32)[:, None]
        cd = (np.exp(-Gt) * (256.0 ** -0.5)).astype(np.float32)[:, None]
        out.append((kd, cd, float(np.exp(GL))))
    return out


CST = {}


def _build_cst():
    cols = []
    off = [0]

    def put(name, arr):
        arr = np.ascontiguousarray(arr, dtype=np.float32)
        assert arr.shape[0] == 128
        CST[name] = (off[0], arr.shape[1])
        cols.append(arr)
        off[0] += arr.shape[1]

    put("ident", np.eye(128))
    put("ones", np.ones((128, 128)))
    for kind, C in (("P", PCHUNK), ("S", 8)):
        cum, grp, negS, negC, row = _masks(C)
        put("mask2" + kind, np.concatenate([cum, grp], 1))
        put("negS" + kind, negS)
        put("negC" + kind, negC)
        rowp = np.zeros((128, 16), np.float32)
        rowp[:, :row.shape[1]] = row
        put("row" + kind, rowp)
        put("caus" + kind, (negC == 0.0).astype(np.float32))
        for hr, (kd, cd, gl) in enumerate(_ret_consts(C)):
            put(f"rkd{kind}{hr}", kd)
            put(f"rcd{kind}{hr}", cd)
            CST[f"rgl{kind}{hr}"] = gl
    return np.concatenate(cols, 1)


CST_ARR = _build_cst()
NCST = CST_ARR.shape[1]


def _rope_tables(pos):
    half = 128
    inv = (np.float32(10000.0) ** (-np.linspace(0.0, 1.0, half, dtype=np.float32))).astype(np.float32)
    ang = (pos.astype(np.float32)[None, :] * inv[:, None]).astype(np.float32)
    return np.cos(ang).astype(np.float32), np.sin(ang).astype(np.float32)


def build_program(debug=None):
    nc = bass.Bass("TRN2", target_bir_lowering=False)
    di = lambda n, s: nc.dram_tensor(n, list(s), F32, kind="ExternalInput").ap()
    do = lambda n, s: nc.dram_tensor(n, list(s), F32, kind="ExternalOutput").ap()
    x_all = di("x_all", (NT * 128, D))
    cst_d = di("cst", (128, NCST))
    cs_d = di("cs_tab", (128, 5, 2, NT * 128))
    w_gdn = di("w_gdn", (8, D, 512))
    w_ba = di("w_ba", (D, 16))
    w_ret = di("w_ret", (4, D, 1024))
    convw_d = di("conv_w", (128, 24, 4))
    small_d = di("small", (128, 128))
    gfin_d = di("gfin", (128, D))
    w_out = di("w_out", (D, D))
    w_mq = di("w_mq", (D, D))
    w_mk = di("w_mk", (D, D))
    w_mv = di("w_mv", (D, D))
    w_mo = di("w_mo", (D, D))
    w_gate = di("w_gate", (D, DFF))
    w_up = di("w_up", (D, DFF))
    w_down = di("w_down", (DFF, D))
    mem_d = di("mem", (256, D))
    ck_d = di("cache_k", (16, 256, D))
    cv_d = di("cache_v", (16, 256, D))
    sg_d = di("st_gdn", (16, 8, 128, 128))
    sr_d = di("st_ret", (16, 4, 256, 256))
    sc_d = di("st_conv", (128, 24, 16, 3))
    y_d = do("y", (TF, D))
    o_sgp = do("o_sgp", (8, 128, 128))
    o_cvp = do("o_cvp", (128, 24, 3))
    o_srp = do("o_srp", (4, 256, 256))
    o_mk = do("o_mk", (256, D))
    o_mv = do("o_mv", (256, D))
    o_sgs = do("o_sgs", (16, 8, 128, 128))
    o_cvs = do("o_cvs", (128, 24, 16, 3))
    o_srs = do("o_srs", (16, 4, 256, 256))
    dbg = {}
    if debug:
        for n, s in debug.items():
            if not n.startswith("_"):
                dbg[n] = do("dbg_" + n, s)

    def stop_at(tag):
        if debug and debug.get("_stop") == tag:
            raise _Stop()

    def dump(name, ap, keys):
        if not (debug and debug.get("_dump")):
            return
        d = nc.dram_tensor("dbg_" + name, list(ap.shape), ap.dtype, kind="ExternalOutput").ap()
        out_toks.append(S.dma("sp", d, ap, "st_dbg", reads=keys))

    es = ExitStack()
    sbt = lambda name, shape, dt: es.enter_context(nc.sbuf_tensor("sb_" + name, list(shape), dt))
    pst = lambda name, shape, dt: es.enter_context(nc.psum_tensor("pp_" + name, list(shape), dt))
    S = Sched(nc)
    S.bank_of.update({k_: "B_" + k_ for k_ in ["ps0", "ps1", "ps2", "ps3", "pT0", "pT1", "P_sc0", "P_sc1", "P_o", "ffn_pg"]})
    S.limit = debug.get("_maxops") if debug else None
    out_toks = []

    cst = sbt("cst", (128, NCST), F32)
    small = sbt("small", (128, 128), F32)
    identb = sbt("identb", (128, 128), BF16)
    onesb = sbt("onesb", (128, 128), BF16)
    hT = sbt("hT", (128, 16, TF), BF16)
    mixT = sbt("mixT", (128, 16, TF), BF16)
    XA = sbt("xarena", (128, 9 * D), F32)
    wbuf = [sbt(f"wbuf{i}", (128, 16, 512), BF16) for i in range(2)]
    WA = sbt("warena", (128, 6160), F32)
    ps = [pst(f"ps{i}", (128, 512), F32) for i in range(8)]

    def C(name):
        o, n = CST[name]
        return cst[:, o:o + n]

    epsc = small[:, 100:104]
    ident = C("ident")
    ones = C("ones")

    def act(out, in_, func, reads, writes, **kw):
        return S.op("act", lambda e: e.activation(out=out, in_=in_, func=func, **kw), reads, writes)

    def mm(out, lhsT, rhs, start, stop, reads, writes, inc=None):
        return S.op("pe", lambda e: e.matmul(out, lhsT=lhsT, rhs=rhs, start=start, stop=stop), reads, writes,
                    inc=stop if inc is None else inc)

    def tr(out, in_, idn, reads, writes, inc=True):
        return S.op("pe", lambda e: e.transpose(out=out, in_=in_, identity=idn), reads, writes, inc=inc)

    def tt(eng, out, in0, in1, op, reads, writes):
        return S.op(eng, lambda e: e.tensor_tensor(out=out, in0=in0, in1=in1, op=op), reads, writes)

    def ts(eng, out, in0, s1, s2, op0, op1, reads, writes):
        if op1 is None:
            return S.op(eng, lambda e: e.tensor_scalar(out=out, in0=in0, scalar1=s1, scalar2=None, op0=op0), reads, writes)
        return S.op(eng, lambda e: e.tensor_scalar(out=out, in0=in0, scalar1=s1, scalar2=s2, op0=op0, op1=op1), reads, writes)

    def stt(eng, out, in0, sc, in1, op0, op1, reads, writes):
        return S.op(eng, lambda e: e.scalar_tensor_tensor(out=out, in0=in0, scalar=sc, in1=in1, op0=op0, op1=op1), reads, writes)

    def cp(eng, out, in_, reads, writes):
        if eng == "act":
            return S.op("act", lambda e: e.copy(out=out, in_=in_), reads, writes)
        return S.op(eng, lambda e: e.tensor_copy(out=out, in_=in_), reads, writes)

    def silu(dst, src, scr, rkeys, skey, wkeys):
        act(scr, src, AF.Exp, rkeys, [skey], scale=-1.0)
        act(scr, scr, AF.Ln, [skey, "small"], [skey], bias=epsc[:, 3:4])
        act(scr, scr, AF.Exp, [skey], [skey], scale=-1.0)
        tt("dve", dst, src, scr, ALU.mult, list(rkeys) + [skey], wkeys)

    def memset(eng, ap, v, writes):
        return S.op(eng, lambda e: e.memset(ap, v), (), writes)

    S.dma("sp", cst[:], cst_d[:, :], "ld_c", writes=["cst"])
    S.dma("sp", small[:], small_d[:, :], "ld_c_sm", writes=["small"])
    cp("dve", identb[:], ident, ["cst"], ["identb"])
    cp("dve", onesb[:], ones, ["cst"], ["identb"])
    KC = ["cst", "identb", "small"]

    wa = Arena(WA, 6160)
    xt = wa.alloc((128, D))
    hb = wa.alloc((128, D), BF16)
    junk = wa.alloc((128, D), BF16)
    stat = wa.alloc((128, 8))
    pT = [ps[6].bitcast(BF16), ps[7].bitcast(BF16)]

    def norm_T(src_ap, src_keys, gcol, dstT, dcol, dkey, load_from=None):
        if load_from is not None:
            S.dma("sp", xt, load_from, "ld_x", writes=["xt"])
            src_ap, src_keys = xt, ["xt"]
        act(junk, src_ap, AF.Square, src_keys, ["stat0"], accum_out=stat[:, 0:1])
        act(stat[:, 1:2], stat[:, 0:1], AF.Ln, ["stat0"], ["stat1"], scale=1.0 / D, bias=epsc[:, 0:1])
        act(stat[:, 2:3], stat[:, 1:2], AF.Exp, ["stat1"], ["stat2"], scale=-0.5)
        ts("dve", hb, src_ap, stat[:, 2:3], None, ALU.mult, None, list(src_keys) + ["stat2"], ["hb"])
        for q in range(4):
            p = pT[q % 2]
            pk = f"pT{q % 2}"
            for j in range(4):
                kc = q * 4 + j
                tr(p[:, j * 128:(j + 1) * 128], hb[:, kc * 128:(kc + 1) * 128], identb[:], ["hb", "identb"], [pk], inc=(j == 3))
            g_b = gcol[:, q * 4:(q + 1) * 4].unsqueeze(2).to_broadcast([128, 4, 128])
            tt("dve", dstT[:, q * 4:(q + 1) * 4, dcol:dcol + 128], p[:, 0:512].rearrange("p (a b) -> p a b", a=4), g_b, ALU.mult,
               [pk, "small"], [dkey])

    wsem = [0]

    def load_w_half(dram_block, idx):
        i, hf = idx // 2, idx % 2
        flat = wbuf[i].rearrange("p a b -> p (a b)")[:, hf * 4096:(hf + 1) * 4096].rearrange("p (a b) -> p a b", a=16)
        S.dma("pool", flat, dram_block.rearrange("(kc p) n -> p kc n", p=128), f"ld_wh{idx}", writes=[f"wbuf{i}h{hf}", f"wbuf{i}"])
        return flat, f"wbuf{i}h{hf}"

    def load_w_half_n(dram_block, idx):
        i, hf = idx // 2, idx % 2
        flat = wbuf[i].rearrange("p a b -> p (a b)")[:, hf * 4096:hf * 4096 + 2048].rearrange("p (a b) -> p a b", a=16)
        S.dma("pool", flat, dram_block.rearrange("(kc p) n -> p kc n", p=128), f"ld_wh{idx}", writes=[f"wbuf{i}h{hf}", f"wbuf{i}"])
        return flat, f"wbuf{i}h{hf}"

    def load_w(dram_block, rows_kc, ncols, slot=None):
        if slot is None:
            i = wsem[0] % 2
            wsem[0] += 1
        else:
            i = slot
        buf = wbuf[i]
        flat = buf.rearrange("p a b -> p (a b)")[:, 0:rows_kc * ncols].rearrange("p (a b) -> p a b", a=rows_kc)
        S.dma("pool", flat, dram_block.rearrange("(kc p) n -> p kc n", p=128), f"ld_w{i}", writes=[f"wbuf{i}"])
        return flat, f"wbuf{i}"

    pacc = [0]

    def next_acc():
        i = pacc[0] % 2
        pacc[0] += 1
        return ps[i], f"ps{i}"

    def proj_fm(w_ap, wkey, fcs, srcT, skey, tokg, evac):
        for fc in fcs:
            for (t0, n) in tokg:
                p, pk = next_acc()
                for kc in range(16):
                    mm(p[:, 0:n], w_ap[:, kc, fc * 128:(fc + 1) * 128], srcT[:, kc, t0:t0 + n], kc == 0, kc == 15,
                       [wkey, skey], [pk])
                evac(fc, t0, n, p[:, 0:n], pk)

    xres = XA.rearrange("p (a b) -> p a b", a=9)

    def xkeys(i):
        return [("xres", i, cg) for cg in range(4)]

    def proj_res(w_dram, nk, srcT, skey, first=False):
        for cg in range(4):
            w_ap, wkey = load_w(w_dram[:, cg * 512:(cg + 1) * 512], nk, 512)
            for i in range(9):
                p, pk = next_acc()
                for kc in range(nk):
                    mm(p[:, :], srcT[:, kc, i * 128:(i + 1) * 128], w_ap[:, kc, :], kc == 0, kc == nk - 1, [wkey, skey], [pk])
                xs = xres[:, i, cg * 512:(cg + 1) * 512]
                if first:
                    S.dma("sp", xs, x_all[(NPRE + i) * 128:(NPRE + i + 1) * 128, cg * 512:(cg + 1) * 512], "ld_xr%d" % ((i * 4 + cg) % 6),
                          writes=[("xres", i, cg)])
                tt("dve", xs, xs, p[:, :], ALU.add, [pk, ("xres", i, cg)], [("xres", i, cg)])

    gmix = small[:, 32:48]
    gmem = small[:, 48:64]
    gcross = small[:, 64:80]
    gffn = small[:, 80:96]

    try:
        for i in range(NPRE):
            norm_T(None, None, gmix, mixT, i * 128, "mixT", load_from=x_all[i * 128:(i + 1) * 128, :])
        for i in range(9):
            norm_T(None, None, gmix, hT, i * 128, "hT", load_from=x_all[(NPRE + i) * 128:(NPRE + i + 1) * 128, :])

        dump("hT", hT[:, :, :], ["hT"])
        dump("hTp", mixT[:, :, 0:1024], ["mixT"])
        stop_at("A")
        S.barrier()
        xa = Arena(XA, 9 * D)
        wm = Arena(WA, 6160)
        u_m = wm.alloc((128, 16, 256), BF16)
        Sr = wm.alloc((128, 4, 512))
        Srb = wm.alloc((128, 4, 512), BF16)
        Sg = wm.alloc((128, 8, 128))
        ba = xa.alloc((128, NT, 16))
        gg = xa.alloc((128, NT, 8))
        beta = xa.alloc((128, NT, 8))
        nbeta = xa.alloc((128, NT, 8))
        hbeta = xa.alloc((128, NT, 8))
        Gtm = xa.alloc((128, NT, 8))
        nGtm = xa.alloc((128, NT, 8))
        bEG = xa.alloc((128, NT, 8))
        eGLG = xa.alloc((128, NT, 8))
        tmpB = xa.alloc((128, NT, 8))
        tmpB2 = xa.alloc((128, NT, 8))
        nexpA = xa.alloc((128, 8))
        gng = xa.alloc((128, 1))
        rng = xa.alloc((128, 8))
        cw = xa.alloc((128, 24, 4))
        ch = xa.alloc((128, 24, 3))
        wba = xa.alloc((128, 16, 16), BF16)
        Sgb = xa.alloc((128, 8, 128), BF16)
        class _Obj:
            pass

        mark = xa.off
        STR = []
        for sid in range(2):
            B = _Obj()
            B.sid = sid
            bk = [ps[2 + 3 * sid], ps[3 + 3 * sid], ps[4 + 3 * sid]]
            B.P_nrm = bk[0][:, 0:256]
            B.P_n2 = bk[0][:, 0:128]
            B.P_G = bk[0][:, 256:512]
            B.P_kq = bk[1][:, 0:256]
            B.P_PP = bk[1][:, 0:128]
            B.P_PPT = bk[1][:, 128:256]
            B.P_Y = bk[1][:, 256:512]
            B.P_U = bk[1][:, 256:384]
            B.P_Sg = bk[1][:, 384:512]
            B.P_O = bk[2][:, 0:256]
            bc16 = bk[2].bitcast(BF16)
            B.pTa = bc16[:, 512:768]
            B.pTb = bc16[:, 768:1024]
            B.acc = (ps[sid], f"ps{sid}")
            for nm_, bnk in (("P_nrm", 0), ("P_n2", 0), ("P_G", 0), ("P_kq", 1), ("P_PP", 1), ("P_PPT", 1), ("P_Y", 1), ("P_U", 1),
                             ("P_Sg", 1), ("P_O", 2), ("pTa0", 2), ("pTa1", 2), ("pTb0", 2), ("pTb1", 2)):
                S.bank_of[nm_ + f"_s{sid}"] = f"bank{2 + 3 * sid + bnk}"
            B.qkT = xa.alloc((128, 128), BF16)
            B.kd = xa.alloc((128, 128), BF16)
            B.qdT = xa.alloc((128, 128), BF16)
            B.sqo = xa.alloc((128, 128), BF16)
            B.rso = xa.alloc((128, 128))
            B.to = xa.alloc((128, 128))
            B.th = xa.alloc((128, 128))
            B.zz = xa.alloc((128, 128))
            B.pre = xa.alloc((128, 3, 259))
            B.preS = xa.alloc((128, 3, 16, 11))
            B.post = xa.alloc((128, 3, 256))
            B.thb = xa.alloc((128, 1, 256))
            B.zb = xa.alloc((128, 256))
            B.sq2 = xa.alloc((128, 256), BF16)
            B.rs2 = xa.alloc((128, 256))
            B.qkTb = xa.alloc((128, 256), BF16)
            B.vTb = xa.alloc((128, 128), BF16)
            B.gcum = xa.alloc((128, 256))
            B.t1 = xa.alloc((128, 128))
            B.t2 = xa.alloc((128, 128))
            B.E1 = xa.alloc((128, 128))
            B.E2 = xa.alloc((128, 128))
            B.Pb = [xa.alloc((128, 128), BF16) for _ in range(2)]
            B.PTb = [xa.alloc((128, 128), BF16) for _ in range(2)]
            B.Yb = [xa.alloc((128, 256), BF16) for _ in range(2)]
            B.wTb = xa.alloc((128, 128), BF16)
            B.ucT = xa.alloc((128, 128))
            B.eG2 = xa.alloc((128, 256))
            B.uTb = xa.alloc((128, 128), BF16)
            B.Ssm = xa.alloc((128, 8, 128))
            B.Ssmb = xa.alloc((128, 8, 128), BF16)
            B.u_m = u_m[:, :, 0:128] if sid == 0 else xa.alloc((128, 16, 128), BF16)
            STR.append(B)
        gdn_end = xa.off
        xa.off = mark
        qkT = xa.alloc((128, 128), BF16)
        kd = xa.alloc((128, 256), BF16)
        sqo = xa.alloc((128, 256), BF16)
        rso = xa.alloc((128, 128))
        to = xa.alloc((128, 256))
        th = xa.alloc((128, 256))
        zz = xa.alloc((128, 256))
        rb = xa.alloc((128, 4, 2, 256))
        csg = xa.alloc((128, 2, 2, 256))
        rt = [xa.alloc((128, 2, 256)) for _ in range(2)]
        rotb = xa.alloc((128, 2, 2, 256), BF16)
        vb = xa.alloc((128, 2, 128), BF16)
        Srs = [xa.alloc((128, 2, 256)) for _ in range(3)]
        Srsb = [xa.alloc((128, 2, 256), BF16) for _ in range(3)]
        xa.off = max(xa.off, gdn_end)
        P_kq = ps[3][:, 0:256]
        P_S = ps[4][:, :]
        P_O = ps[5][:, 0:256]
        P_n2 = ps[5][:, 384:512]
        pTa = pT[0]
        pTb = pT[1]
        S.bank_of.update({"P_kq": "bank3", "P_S": "bank4", "P_O": "bank5", "P_n2": "bank5",
                          "pTa0": "bank6", "pTb0": "bank7"})
        S.cap_shared = {"cst", "identb", "small", "hT", "mixT", "cw", "ch", "wba", "wbuf0", "wbuf1", "ps0", "ps1", "Sg", "Sgb",
                        "gg", "beta", "nbeta", "hbeta", "Gtm", "nGtm", "bEG", "eGLG", "gng", "rng"}

        S.dma("sp", cw, convw_d[:, :, :], "ld_c_cw", writes=["cw"])
        S.dma("pool", wba, w_ba.rearrange("(kc p) n -> p kc n", p=128), "ld_c2", writes=["wba"])
        memset("dve", ch, 0.0, ["ch"])
        memset("dve", Sg, 0.0, [("Sg", h) for h in range(8)])
        memset("dve", Sgb, 0.0, [("Sgb", h) for h in range(8)])
        memset("dve", Sr, 0.0, [("Sr", h) for h in range(4)])
        memset("dve", Srb, 0.0, [("Srb", h) for h in range(4)])
        memset("dve", u_m, 0.0, ["u_m"])

        for ti in range(NT):
            src, skey, c0 = (mixT, "mixT", ti * 128) if ti < NPRE else (hT, "hT", (ti - NPRE) * 128)
            pp = ps[2][:, ti * 16:(ti + 1) * 16]
            for kc in range(16):
                mm(pp, src[:, kc, c0:c0 + 128], wba[:, kc, :], kc == 0, kc == 15, [skey, "wba"], ["ps2"])
        cp("dve", ba.rearrange("p a b -> p (a b)"), ps[2][:, 0:NT * 16], ["ps2"], ["ba"])
        bv = ba[:, :, 0:8]
        av = ba[:, :, 8:16]
        act(tmpB, bv, AF.Exp, ["ba"], ["tmpB"], scale=-1.0)
        ts("dve", tmpB, tmpB, 1.0, None, ALU.add, None, ["tmpB"], ["tmpB"])
        S.op("dve", lambda e: e.reciprocal(out=beta, in_=tmpB), ["tmpB"], ["beta"])
        ts("dve", nbeta, beta, -1.0, None, ALU.mult, None, ["beta"], ["nbeta"])
        ts("dve", hbeta, beta, 0.5, None, ALU.mult, None, ["beta"], ["hbeta"])
        tt("dve", tmpB, av, small[:, 8:16].unsqueeze(1).to_broadcast([128, NT, 8]), ALU.add, ["ba", "small", "beta"], ["tmpB"])
        ts("dve", tmpB2, tmpB, -1.0, None, ALU.mult, None, ["tmpB"], ["tmpB2"])
        tt("dve", tmpB2, tmpB2, tmpB, ALU.max, ["tmpB", "tmpB2"], ["tmpB2"])
        act(tmpB2, tmpB2, AF.Exp, ["tmpB2"], ["tmpB2"], scale=-1.0)
        act(tmpB2, tmpB2, AF.Ln, ["tmpB2", "small"], ["tmpB2"], bias=epsc[:, 3:4])
        stt("dve", tmpB, tmpB, 0.0, tmpB2, ALU.max, ALU.add, ["tmpB", "tmpB2"], ["tmpB"])
        act(nexpA, small[:, 0:8], AF.Exp, ["small"], ["nexpA"])
        ts("dve", nexpA, nexpA, -1.0, None, ALU.mult, None, ["nexpA"], ["nexpA"])
        tt("dve", gg, tmpB, nexpA.unsqueeze(1).to_broadcast([128, NT, 8]), ALU.mult, ["tmpB", "nexpA"], ["gg"])
        ts("dve", gng, small[:, 16:17], 128.0 ** 0.5, None, ALU.mult, None, ["small"], ["gng"])
        ts("dve", rng, small[:, 17:25], 16.0, None, ALU.mult, None, ["small"], ["rng"])
        for ti in range(NT):
            kind = "S" if ti == NT - 1 else "P"
            m2 = C("mask2" + kind)
            mm(ps[3][:, ti * 8:(ti + 1) * 8], m2[:, 0:128], gg[:, ti, :], True, True, ["gg", "cst"], ["ps3"])
            mm(ps[3][:, 256 + ti * 8:256 + (ti + 1) * 8], m2[:, 128:256], gg[:, ti, :], True, True, ["gg", "cst"], ["ps3"])
        f2 = lambda a: a.rearrange("p a b -> p (a b)")
        cp("dve", f2(Gtm), ps[3][:, 0:NT * 8], ["ps3"], ["Gtm"])
        ts("dve", f2(nGtm), ps[3][:, 0:NT * 8], -1.0, None, ALU.mult, None, ["ps3"], ["nGtm"])
        act(f2(tmpB), f2(Gtm), AF.Exp, ["Gtm"], ["tmpB"])
        tt("dve", f2(bEG), f2(tmpB), f2(beta), ALU.mult, ["tmpB", "beta"], ["bEG"])
        tt("dve", f2(tmpB2), ps[3][:, 256:256 + NT * 8], f2(Gtm), ALU.subtract, ["ps3", "Gtm"], ["tmpB2"])
        act(f2(eGLG), f2(tmpB2), AF.Exp, ["tmpB2"], ["eGLG"])
        for nm_, ap_ in (("beta", beta), ("gg", gg), ("Gtm", Gtm), ("bEG", bEG), ("eGLG", eGLG)):
            dump(nm_, ap_, [nm_])
        stop_at("B")
        SCK = ["gg", "beta", "nbeta", "hbeta", "Gtm", "nGtm", "bEG", "eGLG", "gng", "rng"]

        def gdn_tile(B, h, ti, kind, qv, kv, vv, zv, so, mcol, inkeys):
            G_ = 128 // PCHUNK if kind == "P" else 16
            Cg = 128 // G_
            L = {64: 6, 128: 7}[PCHUNK] if kind == "P" else 3
            lo = 128 if so else 0
            col = lambda arr: arr[:, ti, h:h + 1]
            if not so:
                act(B.sq2[:, 0:128], qv, AF.Square, inkeys, ["sq2q"])
            act(B.sq2[:, 128:256], kv, AF.Square, inkeys, ["sq2k"])
            mm(B.P_nrm[:, lo:256], onesb[:], B.sq2[:, lo:256], True, True, ["sq2q", "sq2k", "identb"], ["P_nrm"])
            act(B.rs2[:, lo:256], B.P_nrm[:, lo:256], AF.Ln, ["P_nrm", "small"], ["rs2raw"], bias=epsc[:, 0:1])
            act(B.rs2[:, lo:256], B.rs2[:, lo:256], AF.Exp, ["rs2raw"], ["rs2"], scale=-0.5)
            if not so:
                stt("dve", B.qkTb[:, 0:128], qv, 128.0 ** -0.5, B.rs2[:, 0:128], ALU.mult, ALU.mult, inkeys + ["rs2"], ["qTb"])
            tt("dve", B.qkTb[:, 128:256], kv, B.rs2[:, 128:256], ALU.mult, inkeys + ["rs2"], ["kTb"])
            cp("act", B.vTb, vv, inkeys, ["vTb"])
            tr(B.pTa[:, 0:128], B.qkTb[:, 128:256], identb[:], ["kTb", "identb"], ["pTa0"])
            tr(B.pTa[:, 128:256], B.vTb, identb[:], ["vTb", "identb"], ["pTa1"])
            act(B.kd[:, 0:128], B.pTa[:, 0:128], AF.Copy, ["pTa0"] + SCK, ["kd"], scale=col(eGLG))
            act(B.Yb[0][:, 128:256], B.pTa[:, 0:128], AF.Copy, ["pTa0"] + SCK, ["Y0b"], scale=col(bEG))
            act(B.Yb[0][:, 0:128], B.pTa[:, 128:256], AF.Copy, ["pTa1"] + SCK, ["Y0a"], scale=col(beta))
            ts("dve", B.gcum, C("mask2" + kind), col(gg), None, ALU.mult, None, ["cst"] + SCK, ["gcum"])
            mm(B.P_G, ones, B.gcum, True, True, ["gcum", "cst"], ["P_G"])
            stt("dve", B.t1, B.P_G[:, 0:128], -1.0, C("negS" + kind), ALU.mult, ALU.add, ["P_G", "cst"], ["t1"])
            act(B.E1, B.t1, AF.Exp, ["t1"] + SCK, ["E1"], bias=col(Gtm))
            if not so:
                tt("dve", B.t2, B.P_G[:, 0:128], C("negC" + kind), ALU.add, ["P_G", "cst"], ["t2"])
                act(B.E2, B.t2, AF.Exp, ["t2"] + SCK, ["E2"], bias=col(nGtm))
            act(B.eG2, B.P_G, AF.Exp, ["P_G"], ["eG2"])
            mm(B.P_kq[:, lo:256], B.qkTb[:, 128:256], B.qkTb[:, lo:256], True, True, ["qTb", "kTb"], ["P_kq"])
            stt("dve", B.Pb[0], B.P_kq[:, 128:256], col(nbeta), B.E1, ALU.mult, ALU.mult, ["P_kq", "E1"] + SCK, ["P0"])
            if not so:
                tt("dve", B.qkT, B.P_kq[:, 0:128], B.E2, ALU.mult, ["P_kq", "E2"], ["qkT"])
                tt("dve", B.qdT[:, 0:128], B.qkTb[:, 0:128], B.eG2[:, 0:128], ALU.mult, ["qTb", "eG2"], ["qdT"])
            tr(B.pTb[:, 0:128], B.Pb[0], identb[:], ["P0", "identb"], ["pTb0"])
            cp("act", B.PTb[0], B.pTb[:, 0:128], ["pTb0"], ["PT0"])
            Yk = lambda i: [f"Y{i}a", f"Y{i}b"]
            for j in range(L):
                cur, nxt = j % 2, 1 - (j % 2)
                if j < L - 1:
                    mm(B.P_Y, identb[:], B.Yb[cur], True, False, Yk(cur) + ["identb"], ["P_Y"])
                    mm(B.P_Y, B.PTb[cur], B.Yb[cur], False, True, Yk(cur) + [f"PT{cur}"], ["P_Y"])
                    cp("act", B.Yb[nxt], B.P_Y, ["P_Y"], Yk(nxt))
                    mm(B.P_PPT, B.Pb[cur], B.PTb[cur], True, True, [f"P{cur}", f"PT{cur}"], ["P_PPT"])
                    cp("dve", B.PTb[nxt], B.P_PPT, ["P_PPT"], [f"PT{nxt}"])
                    if j < L - 2:
                        mm(B.P_PP, B.PTb[cur], B.Pb[cur], True, True, [f"P{cur}", f"PT{cur}"], ["P_PP"])
                        cp("dve", B.Pb[nxt], B.P_PP, ["P_PP"], [f"P{nxt}"])
                else:
                    for hf in range(2):
                        mm(B.P_Y[:, hf * 128:(hf + 1) * 128], B.Yb[cur][:, hf * 128:(hf + 1) * 128], identb[:], True, False,
                           Yk(cur) + ["identb"], ["P_Y"], inc=False)
                        mm(B.P_Y[:, hf * 128:(hf + 1) * 128], B.Yb[cur][:, hf * 128:(hf + 1) * 128], B.PTb[cur], False, True,
                           Yk(cur) + [f"PT{cur}"], ["P_Y"], inc=(hf == 1))
                    cp("act", B.ucT, B.P_Y[:, 0:128], ["P_Y"], ["ucT"])
                    cp("dve", B.wTb, B.P_Y[:, 128:256], ["P_Y"], ["wTb"])
            rowm = C("row" + kind)
            chain = kind == "P"
            if chain:
                Sf = [Sg[:, h, :]] * G_
                Sb = [Sgb[:, h, :]] * G_
                skf = [("Sg", h)] * G_
                skb = [("Sgb", h)] * G_
            else:
                Sf = [B.Ssm[:, g % 8, :] for g in range(G_)]
                Sb = [B.Ssmb[:, g % 8, :] for g in range(G_)]
                skf = ["Ssm"] * G_
                skb = ["Ssmb"] * G_

            def part1(g):
                cols = slice(g * Cg, (g + 1) * Cg)
                mm(B.P_U[:, cols], Sb[g], B.wTb[:, cols], True, True, [skb[g], "wTb"], ["P_U"])
                tt("dve", B.uTb[:, cols], B.ucT[:, cols], B.P_U[:, cols], ALU.subtract, ["ucT", "P_U"], ["uTb"])

            def part2(g):
                cols = slice(g * Cg, (g + 1) * Cg)
                if not so:
                    mm(B.P_O[:, cols], Sb[g], B.qdT[:, cols], True, False, [skb[g], "qdT"], ["P_O"])
                    mm(B.P_O[:, cols], B.u_m[:, g, 0:128], B.qkT[:, cols], False, True, ["u_m", "qkT"], ["P_O"])
                mm(B.P_Sg, B.kd[:, 0:128], B.u_m[:, g, 0:128], True, True, ["kd", "u_m"], ["P_Sg"])
                stt("dve", Sf[g], Sf[g], B.eG2[:, 128 + g * Cg:129 + g * Cg], B.P_Sg, ALU.mult, ALU.add, [skf[g], "eG2", "P_Sg"], [skf[g]])
                cp("act", Sb[g], Sf[g], [skf[g]], [skb[g]])

            if chain:
                for g in range(G_):
                    part1(g)
                    tr(B.pTb[:, 128:256], B.uTb, identb[:], ["uTb", "identb"], ["pTb1"])
                    ts("dve", B.u_m[:, g, 0:128], B.pTb[:, 128:256], rowm[:, g:g + 1], None, ALU.mult, None, ["pTb1", "cst"], ["u_m"])
                    part2(g)
            else:
                for hf in range(2):
                    gs = range(hf * 8, hf * 8 + 8)
                    S.dma("sp", B.Ssm, sg_d[hf * 8:(hf + 1) * 8, h].rearrange("s k v -> k s v"), "ld_s", writes=["Ssm"])
                    cp("act", B.Ssmb, B.Ssm, ["Ssm"], ["Ssmb"])
                    for g in gs:
                        part1(g)
                    tr(B.pTb[:, 128:256], B.uTb, identb[:], ["uTb", "identb"], ["pTb1"])
                    tt("dve", B.u_m[:, hf * 8:(hf + 1) * 8, 0:128], B.pTb[:, 128:256].unsqueeze(1).to_broadcast([128, 8, 128]),
                       rowm[:, hf * 8:(hf + 1) * 8].unsqueeze(2).to_broadcast([128, 8, 128]), ALU.mult, ["pTb1", "cst"], ["u_m"])
                    for g in gs:
                        part2(g)
                    S.dma_out("sp", o_sgs[hf * 8:(hf + 1) * 8, h].rearrange("s k v -> k s v"), B.Ssm, "st_s", reads=["Ssm"])
            if so:
                return
            act(B.sqo[:, 0:128], B.P_O[:, 0:128], AF.Square, ["P_O"], ["sqo"])
            mm(B.P_n2, onesb[:], B.sqo[:, 0:128], True, True, ["sqo", "identb"], ["P_n2"])
            act(B.rso, B.P_n2, AF.Ln, ["P_n2", "small"], ["rsoraw"], bias=epsc[:, 1:2])
            act(B.rso, B.rso, AF.Exp, ["rsoraw"], ["rso"], scale=-0.5)
            tt("dve", B.to[:, 0:128], B.P_O[:, 0:128], B.rso, ALU.mult, ["P_O", "rso"], ["to"])
            silu(B.zz[:, 0:128], zv, B.th[:, 0:128], inkeys, "th", ["zz"])
            stt("dve", mixT[:, h, mcol:mcol + 128], B.to[:, 0:128], gng[:, 0:1], B.zz[:, 0:128], ALU.mult, ALU.mult,
                ["to", "zz", "gng"], ["mixT"])

        def gdn_head(B, h, pas):
            so = pas == "pre"
            if so:
                w_ap, wkey = load_w(w_gdn[h][:, 0:384], 16, 384, slot=B.sid)
                parts = [(1, 1), (2, 2)]
                srcT, skey = mixT, "mixT"
                groups = [("P", g_ * 256, 256, 2 * g_) for g_ in range(4)]
            else:
                w_ap, wkey = load_w(w_gdn[h], 16, 512, slot=B.sid)
                parts = [(0, 0), (1, 1), (2, 2), (3, 3)]
                srcT, skey = hT, "hT"
                groups = [("P", g_ * 256, 256, NPRE + 2 * g_) for g_ in range(4)] + [("S", 1024, 128, NPRE + 8)]
            for (kind, t0, n, ti0) in groups:
                cparts = [p for p, _ in parts if p < 3]
                if kind == "P":
                    for p in cparts:
                        cp("dve", B.pre[:, p, 0:3], ch[:, p * 8 + h, :], ["ch"], ["pre"])
                else:
                    for p in cparts:
                        S.dma("sp", B.preS[:, p, :, 0:3], sc_d[:, p * 8 + h, :, :], "ld_cvs", writes=["preS"])
                for (p, wc) in parts:
                    pp, pk = B.acc
                    for kc in range(16):
                        mm(pp[:, 0:n], w_ap[:, kc, wc * 128:(wc + 1) * 128], srcT[:, kc, t0:t0 + n], kc == 0, kc == 15, [wkey, skey], [pk])
                    if p == 3:
                        cp("act", B.zb[:, 0:n], pp[:, 0:n], [pk], ["zb"])
                    elif kind == "P":
                        cp("act", B.pre[:, p, 3:3 + n], pp[:, 0:n], [pk], ["pre"])
                    else:
                        cp("act", B.preS[:, p, :, 3:11], pp[:, 0:128].rearrange("p (s t) -> p s t", s=16), [pk], ["preS"])
                for p in cparts:
                    wcol = lambda i: cw[:, p * 8 + h, i:i + 1]
                    if kind == "P":
                        src = lambda i: B.pre[:, p, i:i + n]
                        dst = B.post[:, p, 0:n]
                        rk = ["pre", "cw"]
                    else:
                        src = lambda i: B.preS[:, p, :, i:i + 8]
                        dst = B.post[:, p, 0:128].rearrange("p (s t) -> p s t", s=16)
                        rk = ["preS", "cw"]
                    ts("dve", dst, src(0), wcol(0), None, ALU.mult, None, rk, [("post", p)])
                    for i in range(1, 4):
                        stt("dve", dst, src(i), wcol(i), dst, ALU.mult, ALU.add, rk + [("post", p)], [("post", p)])
                    silu(B.post[:, p, 0:n], B.post[:, p, 0:n], B.thb[:, 0, 0:n], [("post", p)], "thb", [("post", p)])
                    if kind == "P":
                        cp("dve", ch[:, p * 8 + h, :], B.pre[:, p, n:n + 3], ["pre"], ["ch"])
                    else:
                        S.dma_out("sp", o_cvs[:, p * 8 + h, :, :], B.preS[:, p, :, 8:11], "st_cv", reads=["preS"])
                inkeys = [("post", p) for p in cparts] + (["zb"] if not so else [])
                for j in range(n // 128):
                    c0 = j * 128
                    gdn_tile(B, h, ti0 + j, kind, B.post[:, 0, c0:c0 + 128], B.post[:, 1, c0:c0 + 128], B.post[:, 2, c0:c0 + 128],
                             B.zb[:, c0:c0 + 128], so, t0 + c0, inkeys)
            if so:
                pp, pk = B.acc
                for kc in range(16):
                    mm(pp[:, 0:3], w_ap[:, kc, 0:128], mixT[:, kc, 1021:1024], kc == 0, kc == 15, [wkey, "mixT"], [pk])
                cp("act", ch[:, h, :], pp[:, 0:3], [pk], ["ch"])
            if not so:
                S.dma_out("sp", o_sgp[h], Sg[:, h, :], "st_s2", reads=[("Sg", h)])

        def ret_tile(hr, kind, c0, so, mcol, Sf, Sb, skf, skb):
            G_ = 128 // PCHUNK if kind == "P" else 16
            Cg = 128 // G_
            gl = CST[f"rgl{kind}{hr}"]
            cols_t = slice(c0, c0 + 128)
            rowm = C("row" + kind)
            tr(pTa[:, 0:128], rotb[:, 1, 0, cols_t], identb[:], ["rotk", "identb"], ["pTa0"], inc=False)
            tr(pTa[:, 128:256], rotb[:, 1, 1, cols_t], identb[:], ["rotk", "identb"], ["pTa0"])
            act(kd, pTa[:, 0:256], AF.Copy, ["pTa0", "cst"], ["kd"], scale=C(f"rkd{kind}{hr}"))
            cp("act", vb, rb[:, 2, :, cols_t], ["rb2"], ["vb"])
            tr(pTb[:, 0:128], vb[:, 0, :], identb[:], ["vb", "identb"], ["pTb0"], inc=False)
            tr(pTb[:, 128:256], vb[:, 1, :], identb[:], ["vb", "identb"], ["pTb0"])
            if kind == "P":
                for g in range(G_):
                    ts("dve", u_m[:, g, :], pTb[:, 0:256], rowm[:, g:g + 1], None, ALU.mult, None, ["pTb0", "cst"], ["u_m"])
            else:
                tt("dve", u_m[:, :, :], pTb[:, 0:256].unsqueeze(1).to_broadcast([128, 16, 256]),
                   rowm[:, 0:16].unsqueeze(2).to_broadcast([128, 16, 256]), ALU.mult, ["pTb0", "cst"], ["u_m"])
            if not so:
                mm(P_kq[:, 0:128], rotb[:, 1, 0, cols_t], rotb[:, 0, 0, cols_t], True, False, ["rotk", "rotq"], ["P_kq"])
                mm(P_kq[:, 0:128], rotb[:, 1, 1, cols_t], rotb[:, 0, 1, cols_t], False, True, ["rotk", "rotq"], ["P_kq"])
                stt("dve", qkT, P_kq[:, 0:128], C(f"rcd{kind}{hr}"), C("caus" + kind), ALU.mult, ALU.mult, ["P_kq", "cst"], ["qkT"])
            for g in range(G_):
                cols = slice(g * Cg, (g + 1) * Cg)
                sf, sb, kf, kb = Sf(g), Sb(g), skf(g), skb(g)
                if not so:
                    for dvc in range(2):
                        dsl = slice(dvc * 128, (dvc + 1) * 128)
                        oc = slice(dvc * 128 + g * Cg, dvc * 128 + (g + 1) * Cg)
                        mm(P_O[:, oc], sb[:, 0, dsl], rotb[:, 0, 0, c0 + g * Cg:c0 + (g + 1) * Cg], True, False, [kb, "rotq"], ["P_O"])
                        mm(P_O[:, oc], sb[:, 1, dsl], rotb[:, 0, 1, c0 + g * Cg:c0 + (g + 1) * Cg], False, False, [kb, "rotq"], ["P_O"])
                        mm(P_O[:, oc], u_m[:, g, dsl], qkT[:, cols], False, True, ["u_m", "qkT"], ["P_O"])
                for dkc in range(2):
                    mm(P_S[:, dkc * 256:(dkc + 1) * 256], kd[:, dkc * 128:(dkc + 1) * 128], u_m[:, g, :], True, True, ["kd", "u_m"], ["P_S"],
                       inc=(dkc == 1))
                sff = sf.rearrange("p a b -> p (a b)")
                stt("dve", sff, sff, gl, P_S, ALU.mult, ALU.add, [kf, "P_S"], [kf])
                cp("act", sb.rearrange("p a b -> p (a b)"), sff, [kf], [kb])
                if kind == "S":
                    ret_sample_done(g)
            if so:
                return
            act(sqo, P_O, AF.Square, ["P_O"], ["sqo"])
            mm(P_n2, onesb[:], sqo[:, 0:128], True, False, ["sqo", "identb"], ["P_n2"])
            mm(P_n2, onesb[:], sqo[:, 128:256], False, True, ["sqo", "identb"], ["P_n2"])
            act(rso, P_n2, AF.Ln, ["P_n2", "small"], ["rsoraw"], bias=epsc[:, 2:3])
            act(rso, rso, AF.Exp, ["rsoraw"], ["rso"], scale=-0.5)
            tt("dve", to.rearrange("p (a b) -> p a b", a=2), P_O.rearrange("p (a b) -> p a b", a=2),
               rso.unsqueeze(1).to_broadcast([128, 2, 128]), ALU.mult, ["P_O", "rso"], ["to"])
            rgv = rb[:, 3, :, cols_t]
            silu(zz.rearrange("p (a b) -> p a b", a=2), rgv, th.rearrange("p (a b) -> p a b", a=2), ["rb3"], "th", ["zz"])
            for dvc in range(2):
                stt("dve", mixT[:, 8 + 2 * hr + dvc, mcol:mcol + 128], to[:, dvc * 128:(dvc + 1) * 128], rng[:, 2 * hr + dvc:2 * hr + dvc + 1],
                    zz[:, dvc * 128:(dvc + 1) * 128], ALU.mult, ALU.mult, ["to", "zz", "rng"], ["mixT"])

        ret_ctx = {}

        def ret_sample_done(g):
            hr = ret_ctx["hr"]
            i = g % 3
            out_toks.append(S.dma("sp", o_srs[g, hr].rearrange("(i two) v -> i two v", two=2), Srs[i], "st_rs%d" % i, reads=[("Srs", i)]))

        def ret_head(hr, pas):
            so = pas == "pre"
            ret_ctx["hr"] = hr
            if so:
                loads = [(w_ret[hr][:, 256:768], [(1, 0, 0), (1, 1, 1), (2, 0, 2), (2, 1, 3)])]
                srcT, skey = mixT, "mixT"
                groups = [("P", g * 256, 256, g * 256) for g in range(4)]
            else:
                loads = [(w_ret[hr][:, 0:512], [(0, 0, 0), (0, 1, 1), (1, 0, 2), (1, 1, 3)]),
                         (w_ret[hr][:, 512:1024], [(2, 0, 0), (2, 1, 1), (3, 0, 2), (3, 1, 3)])]
                srcT, skey = hT, "hT"
                groups = [("P", g * 256, 256, (NPRE * 128) + g * 256) for g in range(4)] + [("S", 1024, 128, (NPRE + 8) * 128)]
            wl = [load_w(blk, 16, 512) for blk, _ in loads]
            for (kind, t0, n, tabc) in groups:
                S.dma("sp", csg[:, 0, :, 0:n], cs_d[:, 1 + hr, :, tabc:tabc + n], "ld_cs", writes=["csg"])
                S.dma("sp", csg[:, 1, :, 0:n], cs_d[:, 0, :, tabc:tabc + n], "ld_cs", writes=["csg"])
                for (w_ap, wkey), (_, plist) in zip(wl, loads):
                    for (p, c, wc) in plist:
                        pp, pk = next_acc()
                        for kc in range(16):
                            mm(pp[:, 0:n], w_ap[:, kc, wc * 128:(wc + 1) * 128], srcT[:, kc, t0:t0 + n], kc == 0, kc == 15, [wkey, skey], [pk])
                        cp("act", rb[:, p, c, 0:n], pp[:, 0:n], [pk], [f"rb{p}"])
                pl = [1] if so else [0, 1]
                a, b = pl[0], pl[-1] + 1
                e = rb[:, a:b, 0, 0:n]
                o = rb[:, a:b, 1, 0:n]
                npq = b - a
                cosb = csg[:, a:b, 0, 0:n]
                sinb = csg[:, a:b, 1, 0:n]
                rkeys = [f"rb{p}" for p in pl] + ["csg"]
                tv = [rt[i][:, 0:npq, 0:n] for i in range(2)]
                wk = ["rotq", "rotk"] if not so else ["rotk"]
                tt("dve", tv[0], e, cosb, ALU.mult, rkeys, ["rt0"])
                tt("dve", tv[1], o, sinb, ALU.mult, rkeys, ["rt1"])
                tt("dve", rotb[:, a:b, 0, 0:n], tv[0], tv[1], ALU.subtract, ["rt0", "rt1"], wk)
                tt("dve", tv[0], o, cosb, ALU.mult, rkeys, ["rt0"])
                tt("dve", tv[1], e, sinb, ALU.mult, rkeys, ["rt1"])
                tt("dve", rotb[:, a:b, 1, 0:n], tv[0], tv[1], ALU.add, ["rt0", "rt1"], wk)
                for j in range(n // 128):
                    c0 = j * 128
                    if kind == "P":
                        ret_tile(hr, kind, c0, so, t0 + c0, lambda g: Sr[:, hr, :].rearrange("p (a b) -> p a b", a=2),
                                 lambda g: Srb[:, hr, :].rearrange("p (a b) -> p a b", a=2), lambda g: ("Sr", hr), lambda g: ("Srb", hr))
                    else:
                        ret_tile_sample(hr, c0, t0 + c0)
            if not so:
                out_toks.append(S.dma("sp", o_srp[hr].rearrange("(i two) v -> i two v", two=2),
                                      Sr[:, hr, :].rearrange("p (a b) -> p a b", a=2), "st_s3", reads=[("Sr", hr)]))

        def ret_tile_sample(hr, c0, mcol):
            loaded = set()

            def ensure(g):
                if g in loaded:
                    return
                loaded.add(g)
                i = g % 3
                S.dma("sp", Srs[i], sr_d[g, hr].rearrange("(i two) v -> i two v", two=2), "ld_rs%d" % i, writes=[("Srs", i)])
                cp("act", Srsb[i], Srs[i], [("Srs", i)], [("Srsb", i)])

            def Sf(g):
                ensure(g)
                return Srs[g % 3]

            def Sb(g):
                ensure(g)
                return Srsb[g % 3]

            ret_tile(hr, "S", c0, False, mcol, Sf, Sb, lambda g: ("Srs", g % 3), lambda g: ("Srsb", g % 3))

        if not (debug and debug.get("_nomixer")):
            def gdn_section(pas):
                S.barrier()
                for B in STR:
                    memset("dve", B.uTb, 0.0, ["x"])
                    memset("dve", B.pre, 0.0, ["x"])
                    memset("dve", B.u_m, 0.0, ["x"])
                S.barrier()
                for h0 in range(0, NH, 2):
                    caps = []
                    for sid in range(2):
                        if h0 + sid < NH:
                            S.capture, S.cap_sfx = [], f"_s{sid}"
                            gdn_head(STR[sid], h0 + sid, pas)
                            caps.append(S.capture)
                            S.capture = None
                    S.replay(caps)
                S.barrier()

            NH = debug.get("_nh", 8) if debug else 8
            NR = debug.get("_nr", 4) if debug else 4
            gdn_section("pre")
            dump("Sg_pre", Sg, [("Sg", h) for h in range(8)])
            dump("ch_pre", ch, ["ch"])
            stop_at("Gpre")
            for hr in range(NR):
                ret_head(hr, "pre")
            dump("Sr_pre", Sr, [("Sr", h) for h in range(4)])
            stop_at("Rpre")
            gdn_section("full")
            stop_at("Gfull")
            for hr in range(NR):
                ret_head(hr, "full")
            dump("mixT", mixT[:, :, :], ["mixT"])
            stop_at("Rfull")
            out_toks.append(S.dma("sp", o_cvp[:, :, :], ch, "st_cv2", reads=["ch"]))
        else:
            memset("dve", mixT[:], 0.0, ["mixT"])
        if "mixT" in dbg:
            S.barrier()
            for kc in range(16):
                cp("dve", XA[:, 0:TF], mixT[:, kc, :], ["mixT", "dbgx"], ["dbgx"])
                out_toks.append(S.dma("sp", dbg["mixT"][kc], XA[:, 0:TF], "st_dbg", reads=["dbgx"], writes=["dbgx"]))

        S.barrier()
        proj_res(w_out, 16, mixT, "mixT", first=True)

        KTraw = wa.alloc((128, D))
        KT = KTraw.bitcast(BF16).rearrange("p (a b) -> p a b", a=16)
        KTf = KTraw[:, 0:512]
        KTf2 = xt
        for i in range(9):
            norm_T(xres[:, i, :], xkeys(i), gcross, hT, i * 128, "hT")
        S.barrier()
        qT = mixT

        def evac_q(fc_global):
            def f(fc, t0, n, p, pk):
                cp("act", qT[:, fc_global(fc), t0:t0 + n], p, [pk], ["qT"])
            return f

        TOKG = [(0, 512), (512, 512), (1024, 128)]
        for cg in range(4):
            w_ap, wkey = load_w(w_mq[:, cg * 512:(cg + 1) * 512], 16, 512)
            proj_fm(w_ap, wkey, range(4), hT, "hT", TOKG, evac_q(lambda fc, cg=cg: cg * 4 + fc))
        S.barrier()
        ha = Arena(hT.rearrange("p a b -> p (a b)").bitcast(F32), 16 * TF // 2)
        mT = ha.alloc((128, 16, 256), BF16)
        Vb = ha.alloc((128, 2, D), BF16)
        Kb = ha.alloc((128, 2, D), BF16)
        kvo = ha.alloc((128, 512))
        sc = ha.alloc((128, 4, 256))
        pb = ha.alloc((128, 4, 256), BF16)
        pTs = ha.alloc((128, 8, 128), BF16)
        smx = ha.alloc((128, 16))
        oT = XA
        oTb = mT[:, :, 0:128]

        def mem_kv(src_rows_keys):
            pass

        for i in range(2):
            norm_T(None, None, gmem, mT, i * 128, "mT", load_from=mem_d[i * 128:(i + 1) * 128, :])
        for (wd, od, dst, dkey) in ((w_mk, o_mk, Kb, "Kb"), (w_mv, o_mv, Vb, "Vb")):
            for cg in range(4):
                w_ap, wkey = load_w(wd[:, cg * 512:(cg + 1) * 512], 16, 512)
                for i in range(2):
                    p, pk = next_acc()
                    for kc in range(16):
                        mm(p[:, :], mT[:, kc, i * 128:(i + 1) * 128], w_ap[:, kc, :], kc == 0, kc == 15, [wkey, "mT"], [pk])
                    cp("act", kvo, p[:, :], [pk], ["kvo"])
                    cp("dve", dst[:, i, cg * 512:(cg + 1) * 512], kvo, ["kvo"], [dkey])
                    out_toks.append(S.dma("sp", od[i * 128:(i + 1) * 128, cg * 512:(cg + 1) * 512], kvo, "st_kv", reads=["kvo"]))

        def make_KT(src, skey):
            for c4 in range(4):
                for i in range(2):
                    p = pT[(c4 * 2 + i) % 2]
                    pk = f"pT{(c4 * 2 + i) % 2}"
                    for j in range(4):
                        c = c4 * 4 + j
                        tr(p[:, j * 128:(j + 1) * 128], src[:, i, c * 128:(c + 1) * 128], identb[:], [skey, "identb"], [pk], inc=(j == 3))
                    cp("act", KT[:, c4 * 4:(c4 + 1) * 4, i * 128:(i + 1) * 128], p[:, 0:512].rearrange("p (a b) -> p a b", a=4), [pk], ["KT"])

        class _A:
            pass

        A0 = _A()
        A0.sc, A0.pb, A0.pTs, A0.smx, A0.oTb = sc, pb, pTs, smx, oTb
        A0.P_sc, A0.P_o, A0.pTa = [ps[2], ps[3]], ps[4][:, :], pT[0]
        A1 = _A()
        kbf = Kb.rearrange("p a b -> p (a b)").bitcast(F32)
        A1.sc = kbf[:, 0:1024].rearrange("p (a b) -> p a b", a=4)
        A1.pb = kbf[:, 1024:1536].bitcast(BF16).rearrange("p (a b) -> p a b", a=4)
        A1.pTs = kbf[:, 1536:2048].bitcast(BF16).rearrange("p (a b) -> p a b", a=8)
        A1.smx = ha.alloc((128, 16))
        A1.oTb = mT[:, :, 128:256]
        A1.P_sc, A1.P_o, A1.pTa = [ps[0], ps[1]], ps[5][:, :], pT[1]
        P_sc, P_o = A0.P_sc, A0.P_o
        S.bank_of.update({"P_sc0_a0": "B_P_sc0", "P_sc1_a0": "B_P_sc1", "P_o_a0": "B_P_o", "pT0_a0": "B_pT0",
                          "P_sc0_a1": "B_ps0", "P_sc1_a1": "B_ps1", "P_o_a1": "B_bank5", "pT0_a1": "B_pT1"})
        SCALE = 512.0 ** -0.5

        def attn_tile(A, i, seqs):
            tc_ = slice(i * 128, (i + 1) * 128)
            if seqs is None:
                for hh in range(4):
                    pp = A.P_sc[hh // 2][:, (hh % 2) * 256:(hh % 2 + 1) * 256]
                    for c in range(4):
                        mm(pp, qT[:, hh * 4 + c, tc_], KT[:, hh * 4 + c, :], c == 0, c == 3, [("qT", i), "KT"], [f"P_sc{hh // 2}"], inc=(c == 3))
                    cp("act", A.sc[:, hh, :], pp, [f"P_sc{hh // 2}"], ["sc"])
            softmax_and_out(A, i, tc_, None)

        def softmax_and_out(A, i, tc_, vsrc):
            S.op("dve", lambda e: e.reduce_max(out=A.smx[:, 0:4], in_=A.sc[:, :, :], axis=mybir.AxisListType.X), ["sc"], ["smx0"])
            ts("dve", A.smx[:, 4:8], A.smx[:, 0:4], -SCALE, None, ALU.mult, None, ["smx0"], ["smx1"])
            for hh in range(4):
                act(A.sc[:, hh, :], A.sc[:, hh, :], AF.Exp, ["sc", "smx1"], ["sc", "smx2"], scale=SCALE, bias=A.smx[:, 4 + hh:5 + hh], accum_out=A.smx[:, 8 + hh:9 + hh])
            S.op("dve", lambda e: e.reciprocal(out=A.smx[:, 12:16], in_=A.smx[:, 8:12]), ["smx2", "sc"], ["smx3"])
            tt("dve", A.pb, A.sc, A.smx[:, 12:16].unsqueeze(2).to_broadcast([128, 4, 256]), ALU.mult, ["sc", "smx3"], ["pb"])
            for hh in range(4):
                for mc in range(2):
                    tr(A.pTa[:, (hh * 2 + mc) * 128:(hh * 2 + mc + 1) * 128], A.pb[:, hh, mc * 128:(mc + 1) * 128], identb[:], ["pb", "identb"], ["pT0"],
                       inc=(hh == 3 and mc == 1))
            cp("act", A.pTs.rearrange("p a b -> p (a b)"), A.pTa[:, 0:1024], ["pT0"], ["pTs"])

        def attn_out_prompt(A, i):
            tc_ = slice(i * 128, (i + 1) * 128)
            for hh in range(4):
                for c in range(4):
                    oc = slice(c * 128, (c + 1) * 128)
                    for mc in range(2):
                        mm(A.P_o[:, oc], Vb[:, mc, (hh * 4 + c) * 128:(hh * 4 + c + 1) * 128], A.pTs[:, hh * 2 + mc, :], mc == 0, mc == 1, ["Vb", "pTs"], ["P_o"],
                           inc=(c == 3 and mc == 1))
                cp("act", A.oTb[:, hh * 4:(hh + 1) * 4, :], A.P_o.rearrange("p (a b) -> p a b", a=4), ["P_o"], ["oTb"])

        def stash_o(A, i):
            tc_ = slice(i * 128, (i + 1) * 128)
            cp("dve", qT[:, :, tc_], A.oTb[:, :, :], ["oTb", ("qT", i)], [("qT", i)])

        S.barrier()
        make_KT(Kb, "Kb")
        S.barrier()
        S.cap_shared = {"qT", "KT", "Vb", "identb", "cst"}
        for i0 in range(0, 8, 2):
            caps = []
            for sid, A_ in enumerate((A0, A1)):
                S.capture, S.cap_sfx = [], f"_a{sid}"
                attn_tile(A_, i0 + sid, None)
                attn_out_prompt(A_, i0 + sid)
                stash_o(A_, i0 + sid)
                caps.append(S.capture)
                S.capture = None
            S.replay(caps)
        S.barrier()
        A = A0

        mrow = C("rowS")
        i = 8
        tc_ = slice(i * 128, (i + 1) * 128)
        memset("dve", sc, 0.0, ["sc"])
        oacc = ha.alloc((128, 16, 128), BF16) if False else oTb
        for s in range(16):
            kbuf, kkey = (Kb, "Kb") if s % 2 == 0 else (Vb, "Vb")
            S.dma("pool", kbuf, ck_d[s].rearrange("(c p) d -> p c d", p=128), "ld_ck%d" % (s % 2), writes=[kkey])
            make_KT(kbuf, kkey)
            for hh in range(4):
                pp = P_sc[hh // 2][:, (hh % 2) * 256:(hh % 2 + 1) * 256]
                for c in range(4):
                    mm(pp, qT[:, hh * 4 + c, tc_], KT[:, hh * 4 + c, :], c == 0, c == 3, ["qT", "KT"], [f"P_sc{hh // 2}"], inc=(c == 3))
                stt("dve", sc[:, hh, :], pp, mrow[:, s:s + 1], sc[:, hh, :], ALU.mult, ALU.add, [f"P_sc{hh // 2}", "sc", "cst"], ["sc"])
        softmax_and_out(A0, i, tc_, None)
        for s in range(16):
            vbuf, vkey = (Vb, "Vb") if s % 2 == 0 else (Kb, "Kb")
            S.dma("pool", vbuf, cv_d[s].rearrange("(c p) d -> p c d", p=128), "ld_cv%d" % (s % 2), writes=[vkey])
            for hh in range(4):
                for c in range(4):
                    oc = slice(c * 128 + s * 8, c * 128 + s * 8 + 8)
                    for mc in range(2):
                        mm(P_o[:, oc], vbuf[:, mc, (hh * 4 + c) * 128:(hh * 4 + c + 1) * 128], pTs[:, hh * 2 + mc, s * 8:(s + 1) * 8], mc == 0, mc == 1,
                           [vkey, "pTs"], ["P_o"], inc=(c == 3 and mc == 1))
                cp("act", oTb[:, hh * 4:(hh + 1) * 4, s * 8:(s + 1) * 8], P_o.rearrange("p (a b) -> p a b", a=4)[:, :, s * 8:(s + 1) * 8], ["P_o"], ["oTb"])
        stash_o(A0, i)
        S.barrier()
        proj_res(w_mo, 16, qT, "qT")

        S.barrier()
        for i in range(9):
            norm_T(xres[:, i, :], xkeys(i), gffn, hT, i * 128, "hT")
        actT = mixT
        gsb = wa.alloc((128, 512)) if False else None
        ffn_pair = [0]
        for fg in range(4):
            for j in range(11):
                fcg = fg * 11 + j
                if j % 2 == 0:
                    nfc = min(2, 11 - j)
                    pr_ = ffn_pair[0] % 2
                    ffn_pair[0] += 1
                    wg_ap, wgk = load_w_half(w_gate[:, fcg * 128:(fcg + 2) * 128], pr_ * 2) if nfc == 2 else load_w_half_n(w_gate[:, fcg * 128:(fcg + 1) * 128], pr_ * 2)
                    wu_ap, wuk = load_w_half(w_up[:, fcg * 128:(fcg + 2) * 128], pr_ * 2 + 1) if nfc == 2 else load_w_half_n(w_up[:, fcg * 128:(fcg + 1) * 128], pr_ * 2 + 1)
                jj = j % 2
                for (t0, n) in TOKG:
                    pg, pgk = ps[2], "ffn_pg"
                    pu, puk = next_acc()
                    for kc in range(16):
                        mm(pg[:, 0:n], wg_ap[:, kc, jj * 128:(jj + 1) * 128], hT[:, kc, t0:t0 + n], kc == 0, kc == 15, [wgk, wgk[:5], "hT"], [pgk])
                    for kc in range(16):
                        mm(pu[:, 0:n], wu_ap[:, kc, jj * 128:(jj + 1) * 128], hT[:, kc, t0:t0 + n], kc == 0, kc == 15, [wuk, wuk[:5], "hT"], [puk])
                    sg_ = kvo[:, 0:n] if False else None
                    act(KTf[:, 0:n], pg[:, 0:n], AF.Silu, [pgk], ["silu"])
                    tt("dve", actT[:, j, t0:t0 + n], KTf[:, 0:n], pu[:, 0:n], ALU.mult, ["silu", puk], ["actT"])
            for cg in range(4):
                w_ap, wkey = load_w(w_down[fg * 11 * 128:(fg + 1) * 11 * 128, cg * 512:(cg + 1) * 512], 11, 512)
                for i in range(9):
                    p, pk = next_acc()
                    for kc in range(11):
                        mm(p[:, :], actT[:, kc, i * 128:(i + 1) * 128], w_ap[:, kc, :], kc == 0, kc == 10, [wkey, "actT"], [pk])
                    xs = xres[:, i, cg * 512:(cg + 1) * 512]
                    tt("dve", xs, xs, p[:, :], ALU.add, [pk, ("xres", i, cg)], [("xres", i, cg)])

        S.barrier()
        gfin = KTf2
        S.dma("sp", gfin, gfin_d[:, :], "ld_c", writes=["gfin"])
        for i in range(9):
            act(junk, xres[:, i, :], AF.Square, xkeys(i), ["stat0"], accum_out=stat[:, 0:1])
            act(stat[:, 1:2], stat[:, 0:1], AF.Ln, ["stat0"], ["stat1"], scale=1.0 / D, bias=epsc[:, 0:1])
            act(stat[:, 2:3], stat[:, 1:2], AF.Exp, ["stat1"], ["stat2"], scale=-0.5)
            stt("dve", xres[:, i, :], xres[:, i, :], stat[:, 2:3], gfin, ALU.mult, ALU.mult, xkeys(i) + ["stat2", "gfin"], xkeys(i))
            out_toks.append(S.dma("sp", y_d[i * 128:(i + 1) * 128, :], xres[:, i, :], "st_y", reads=xkeys(i)))


    except _Stop:
        pass
    S.limit = None
    if debug:
        print("recorded ops", S.nrec, S.count, flush=True)
    S.wait_all("sp", out_toks + S.out_toks)
    S.emit()
    es.close()
    return nc


_NC_CACHE = {}


def _get_nc():
    if "nc" not in _NC_CACHE:
        _NC_CACHE["nc"] = build_program()
    return _NC_CACHE["nc"]


def _prep_inputs(inp):
    f = lambda k: np.ascontiguousarray(np.asarray(inp[k], dtype=np.float32))
    x_prompt, x_sample, mem_prompt = f("x_prompt"), f("x_sample"), f("mem_prompt")
    ck, cv = f("cache_mem_k")[0], f("cache_mem_v")[0]
    sg, scv, sr = f("state_gdn")[0], f("state_gdn_conv")[0], f("state_ret")[0]
    w_in = f("w_in")[0]
    i0, i1, i3, i4, i5, i6 = 3072, 4096, 4112, 5136, 6160, 7184
    w_gdn = np.stack([np.concatenate([w_in[:, h * 128:(h + 1) * 128], w_in[:, 1024 + h * 128:1024 + (h + 1) * 128],
                                      w_in[:, 2048 + h * 128:2048 + (h + 1) * 128], w_in[:, i0 + h * 128:i0 + (h + 1) * 128]], 1)
                      for h in range(8)])
    w_ba = np.ascontiguousarray(w_in[:, i1:i3])
    perm = np.concatenate([np.arange(0, 256, 2), np.arange(1, 256, 2)])
    w_ret = np.stack([np.concatenate([w_in[:, i3 + hr * 256 + perm], w_in[:, i4 + hr * 256 + perm],
                                      w_in[:, i5 + hr * 256:i5 + (hr + 1) * 256], w_in[:, i6 + hr * 256:i6 + (hr + 1) * 256]], 1)
                      for hr in range(4)])
    conv_w = np.ascontiguousarray(f("conv_w")[0].reshape(24, 128, 4).transpose(1, 0, 2))
    small = np.zeros((128, 128), np.float32)
    small[:, 0:8] = f("gdn_a_log")[0][None, :]
    small[:, 8:16] = f("gdn_dt_bias")[0][None, :]
    small[:, 16] = f("gdn_norm")[0]
    small[:, 17:25] = f("ret_norm")[0].reshape(8, 128).T
    small[:, 32:48] = f("norm_mix")[0].reshape(16, 128).T
    small[:, 48:64] = f("norm_mem_in")[0].reshape(16, 128).T
    small[:, 64:80] = f("norm_cross")[0].reshape(16, 128).T
    small[:, 80:96] = f("norm_ffn")[0].reshape(16, 128).T
    small[:, 100] = EPS
    small[:, 101] = 128 * EPS
    small[:, 102] = 256 * EPS
    small[:, 103] = 1.0
    gfin = np.ascontiguousarray(np.broadcast_to(f("norm_final")[None, :], (128, D)))
    shared = {"cst": CST_ARR, "w_gdn": w_gdn, "w_ba": w_ba, "w_ret": w_ret, "conv_w": conv_w, "small": small, "gfin": gfin,
              "w_out": f("w_out")[0], "w_mq": f("w_mem_q")[0], "w_mk": f("w_mem_k")[0], "w_mv": f("w_mem_v")[0], "w_mo": f("w_mem_o")[0],
              "w_gate": f("w_gate")[0], "w_up": f("w_up")[0], "w_down": f("w_down")[0]}
    maps = []
    for c in range(8):
        b, half = c // 2, c % 2
        pre = x_prompt[b, 0:1024] if half else np.zeros((1024, D), np.float32)
        x_all = np.concatenate([pre, x_prompt[b, half * 1024:(half + 1) * 1024], x_sample[16 * c:16 * c + 16].reshape(128, D)], 0)
        pos = np.concatenate([np.arange(1024), half * 1024 + np.arange(1024), 16384 + (np.arange(128) % 8)]).astype(np.float32)
        cos, sin = _rope_tables(pos)
        tabs = [np.stack([cos, sin], 1)]
        jj = np.concatenate([np.arange(2048) % PCHUNK, np.arange(128) % 8]).astype(np.float32)
        for hr in range(4):
            lg = np.log1p(-np.exp2(np.float32(-5.0 - hr))).astype(np.float32)
            eg = np.exp(lg * (jj + 1.0)).astype(np.float32)[None, :]
            tabs.append(np.stack([cos * eg, sin * eg], 1))
        m = dict(shared)
        m.update({
            "x_all": np.ascontiguousarray(x_all),
            "cs_tab": np.ascontiguousarray(np.stack(tabs, 1)),
            "mem": mem_prompt[b],
            "cache_k": np.ascontiguousarray(ck[16 * c:16 * c + 16].reshape(16, 256, D)),
            "cache_v": np.ascontiguousarray(cv[16 * c:16 * c + 16].reshape(16, 256, D)),
            "st_gdn": np.ascontiguousarray(sg[16 * c:16 * c + 16]),
            "st_ret": np.ascontiguousarray(sr[16 * c:16 * c + 16]),
            "st_conv": np.ascontiguousarray(scv[16 * c:16 * c + 16].transpose(2, 0, 1).reshape(24, 128, 16, 3).transpose(1, 0, 2, 3)),
        })
        maps.append(m)
    return maps


def _assemble(res):
    y_prompt = np.zeros((4, 2048, D), np.float32)
    y_sample = np.zeros((128, 8, D), np.float32)
    sgp = np.zeros((1, 4, 8, 128, 128), np.float32)
    cvp = np.zeros((1, 4, 3, 3072), np.float32)
    srp = np.zeros((1, 4, 4, 256, 256), np.float32)
    mkp = np.zeros((1, 4, 256, 4, 512), np.float32)
    mvp = np.zeros((1, 4, 256, 4, 512), np.float32)
    sgs = np.zeros((1, 128, 8, 128, 128), np.float32)
    cvs = np.zeros((1, 128, 3, 3072), np.float32)
    srs = np.zeros((1, 128, 4, 256, 256), np.float32)
    for c in range(8):
        r = res[c]
        b, half = c // 2, c % 2
        y = np.asarray(r["y"])
        y_prompt[b, half * 1024:(half + 1) * 1024] = y[0:1024]
        y_sample[16 * c:16 * c + 16] = y[1024:1152].reshape(16, 8, D)
        if half == 1:
            sgp[0, b] = np.asarray(r["o_sgp"])
            cvp[0, b] = np.asarray(r["o_cvp"]).transpose(1, 0, 2).reshape(3072, 3).T
            srp[0, b] = np.asarray(r["o_srp"])
        else:
            mkp[0, b] = np.asarray(r["o_mk"]).reshape(256, 4, 512)
            mvp[0, b] = np.asarray(r["o_mv"]).reshape(256, 4, 512)
        sgs[0, 16 * c:16 * c + 16] = np.asarray(r["o_sgs"])
        cvs[0, 16 * c:16 * c + 16] = np.asarray(r["o_cvs"]).transpose(2, 3, 1, 0).reshape(16, 3, 3072)
        srs[0, 16 * c:16 * c + 16] = np.asarray(r["o_srs"])
    return (y_prompt, y_sample, sgp, cvp, srp, mkp, mvp, sgs, cvs, srs)


def kernel(**inputs):
    nc = _get_nc()
    maps = _prep_inputs(inputs)
    r = run_bass_kernel_spmd(nc, maps, core_ids=list(range(8)))
    return _assemble(r.results)
```

```python
import numpy as np
from contextlib import ExitStack
import concourse.bass as bass
import concourse.mybir as mybir
from concourse.bass_utils import run_bass_kernel_spmd

F32 = mybir.dt.float32
BF16 = mybir.dt.bfloat16
AF = mybir.ActivationFunctionType
ALU = mybir.AluOpType

D = 2048
NPRE = 8
NFULL = 8
NT = 17
TF = 1152
DFF = 5632
EPS = 1e-6
NEG = -30000.0
PCHUNK = 128
import os
ATTACH_WAIT = os.environ.get('ATTACH_WAIT', '1') == '1'


class _Stop(Exception):
    pass


class Sched:
    ENGS = ("pe", "act", "dve", "pool", "sp")

    def __init__(self, nc):
        self.nc = nc
        self.ops = {e: [] for e in self.ENGS}
        self.count = {e: 0 for e in self.ENGS}
        self.seen = {e: {} for e in self.ENGS}
        self.keys = {}
        self.dcount = {}
        self.sem_handles = {}
        self.nrec = 0
        self.limit = None
        self.bank_of = {}
        self.capture = None
        self.cap_sfx = ""
        self.cap_shared = set()
        self.out_toks = []

    def _ck(self, keys):
        out = []
        for k_ in keys:
            if k_ in self.cap_shared or (isinstance(k_, tuple) and k_[0] in self.cap_shared):
                out.append(k_)
            elif isinstance(k_, tuple):
                out.append((k_[0] + self.cap_sfx,) + tuple(k_[1:]))
            else:
                out.append(k_ + self.cap_sfx)
        return out

    def replay(self, lists):
        n = max(len(l) for l in lists)
        for i in range(n):
            for l in lists:
                if i < len(l):
                    it = l[i]
                    if it[0] == "op":
                        self.op(*it[1:])
                    else:
                        _, q, out, in_, sem, reads, writes, final, kw = it
                        tok = self.dma(q, out, in_, sem, reads, writes, **kw)
                        if final:
                            self.out_toks.append(tok)

    def dma_out(self, q, out, in_, sem, reads=(), writes=(), **kw):
        if self.capture is not None:
            self.capture.append(("dma", q, out, in_, sem + self.cap_sfx, self._ck(reads), self._ck(writes), True, kw))
            return
        self.out_toks.append(self.dma(q, out, in_, sem, reads, writes, **kw))

    def _tick(self):
        self.nrec += 1
        if self.limit and self.nrec > self.limit:
            raise _Stop()

    def _deps(self, eng, reads, writes):
        toks = {}

        def add(tok, raw):
            if tok is None:
                return
            s, v = tok
            if s == eng and not raw:
                return
            if self.seen[eng].get(s, 0) >= v:
                return
            if toks.get(s, 0) < v:
                toks[s] = v

        for k in reads:
            st = self.keys.get(k)
            if st:
                add(st[0], True)
        for k in writes:
            st = self.keys.get(k)
            if st:
                add(st[0], False)
                for r in st[1]:
                    add(r, False)
        for s, v in toks.items():
            if s in self.count:
                assert v <= self.count[s], f"{eng} waits on pending inc of {s}: {v}>{self.count[s]}"
            self.seen[eng][s] = v
        return list(toks.items())

    def _mark(self, tok, reads, writes):
        for k in reads:
            st = self.keys.setdefault(k, [None, []])
            st[1].append(tok)
        for k in writes:
            self.keys[k] = [tok, []]

    def op(self, eng, fn, reads=(), writes=(), inc=True):
        if self.capture is not None:
            self.capture.append(("op", eng, fn, self._ck(reads), self._ck(writes), inc))
            return None
        self._tick()
        banks = {self.bank_of[k] for k in list(reads) + list(writes) if k in self.bank_of}
        if banks:
            writes = list(writes) + sorted(banks)
        waits = self._deps(eng, reads, writes)
        if inc:
            self.count[eng] += 1
            tok = (eng, self.count[eng])
        else:
            tok = (eng, self.count[eng] + 1)
        self.ops[eng].append((fn, waits, eng if inc else None, 1))
        self._mark(tok, reads, writes)
        return tok

    def dma(self, q, out, in_, sem, reads=(), writes=(), **kw):
        if self.capture is not None:
            self.capture.append(("dma", q, out, in_, sem + self.cap_sfx, self._ck(reads), self._ck(writes), False, kw))
            return None
        self._tick()
        waits = self._deps(q, reads, writes)
        prev = self.dcount.get(sem, 0)
        if prev > self.seen[q].get(sem, 0):
            self.seen[q][sem] = prev
            waits = [w for w in waits if w[0] != sem] + [(sem, prev)]
        self.dcount[sem] = self.dcount.get(sem, 0) + 16
        tok = (sem, self.dcount[sem])
        self.ops[q].append((lambda e: e.dma_start(out=out, in_=in_, **kw), waits, sem, 16))
        self._mark(tok, reads, writes)
        return tok

    def wait_all(self, eng, toks):
        waits = []
        for s, v in toks:
            if self.seen[eng].get(s, 0) < v:
                self.seen[eng][s] = v
                waits.append((s, v))
        self.ops[eng].append((None, waits, None, 0))

    def barrier(self):
        toks = [(e, self.count[e]) for e in self.ENGS if self.count[e] > 0]
        toks += [(s, v) for s, v in self.dcount.items()]
        for e in self.ENGS:
            self.wait_all(e, [t for t in toks if t[0] != e])
        self.keys = {}

    def emit(self):
        nc = self.nc
        names = list(self.ENGS) + sorted(self.dcount.keys())
        with ExitStack() as es:
            for n in names:
                self.sem_handles[n] = es.enter_context(nc.semaphore("s_" + n))
            block = es.enter_context(nc.Block())
            H = self.sem_handles

            def mk(e):
                def body(engine):
                    for fn, waits, incsem, incv in self.ops[e]:
                        if fn is None or not ATTACH_WAIT:
                            for s, v in waits:
                                engine.wait_ge(H[s], v)
                            if fn is None:
                                continue
                            ins = fn(engine)
                        else:
                            for s, v in waits[:-1]:
                                engine.wait_ge(H[s], v)
                            ins = fn(engine)
                            if waits:
                                ins._wait_ge(H[waits[-1][0]], waits[-1][1])
                        if incsem is not None:
                            ins.then_inc(H[incsem], incv)
                return body

            block.tensor(mk("pe"))
            block.scalar(mk("act"))
            block.vector(mk("dve"))
            block.gpsimd(mk("pool"))
            block.sync(mk("sp"))


class Arena:
    def __init__(self, t, n):
        self.t, self.n, self.off = t, n, 0

    def reset(self):
        self.off = 0

    def alloc(self, shape, dt=F32):
        n = int(np.prod(shape[1:]))
        nf = n if dt == F32 else (n + 1) // 2
        assert self.off + nf <= self.n, f"arena overflow {self.off}+{nf}>{self.n}"
        ap = self.t[:, self.off:self.off + nf]
        self.off += nf
        if dt != F32:
            ap = ap.bitcast(dt)[:, 0:n]
        if len(shape) == 3:
            ap = ap.rearrange("p (a b) -> p a b", a=shape[1])
        elif len(shape) == 4:
            ap = ap.rearrange("p (a b c) -> p a b c", a=shape[1], b=shape[2])
        return ap


def _masks(C):
    t = np.arange(128)
    g = t // C
    same = g[:, None] == g[None, :]
    cum = (same & (t[:, None] <= t[None, :])).astype(np.float32)
    grp = same.astype(np.float32)
    negS = np.where(same & (t[:, None] > t[None, :]), 0.0, NEG).astype(np.float32)
    negC = np.where(same & (t[None, :] >= t[:, None]), 0.0, NEG).astype(np.float32)
    G = 128 // C
    row = (g[:, None] == np.arange(G)[None, :]).astype(np.float32)
    return cum, grp, negS, negC, row


def _ret_consts(C):
    t = np.arange(128)
    j = (t % C).astype(np.float32)
    out = []
    for hr in range(4):
        lg = np.log1p(-np.exp2(np.float32(-5.0 - hr))).astype(np.float32)
        Gt = lg * (j + 1.0)
        GL = lg * C
        kd = (np.exp(GL - Gt) * (256.0 ** -0.5)).astype(np.float32)[:, None]
        cd = (np.exp(-Gt) * (256.0 ** -0.5)).astype(np.float32)[:, None]
        out.append((kd, cd, float(np.exp(GL))))
    return out


CST = {}


def _build_cst():
    cols = []
    off = [0]

    def put(name, arr):
        arr = np.ascontiguousarray(arr, dtype=np.float32)
        assert arr.shape[0] == 128
        CST[name] = (off[0], arr.shape[1])
        cols.append(arr)
        off[0] += arr.shape[1]

    put("ident", np.eye(128))
    put("ones", np.ones((128, 128)))
    for kind, C in (("P", PCHUNK), ("S", 8)):
        cum, grp, negS, negC, row = _masks(C)
        put("mask2" + kind, np.concatenate([cum, grp], 1))
        put("negS" + kind, negS)
        put("negC" + kind, negC)
        rowp = np.zeros((128, 16), np.float32)
        rowp[:, :row.shape[1]] = row
        put("row" + kind, rowp)
        put("caus" + kind, (negC == 0.0).astype(np.float32))
        for hr, (kd, cd, gl) in enumerate(_ret_consts(C)):
            put(f"rkd{kind}{hr}", kd)
            put(f"rcd{kind}{hr}", cd)
            CST[f"rgl{kind}{hr}"] = gl
    return np.concatenate(cols, 1)


CST_ARR = _build_cst()
NCST = CST_ARR.shape[1]


def _rope_tables(pos):
    half = 128
    inv = (np.float32(10000.0) ** (-np.linspace(0.0, 1.0, half, dtype=np.float32))).astype(np.float32)
    ang = (pos.astype(np.float32)[None, :] * inv[:, None]).astype(np.float32)
    return np.cos(ang).astype(np.float32), np.sin(ang).astype(np.float32)


def build_program(debug=None):
    nc = bass.Bass("TRN2", target_bir_lowering=False)
    di = lambda n, s: nc.dram_tensor(n, list(s), F32, kind="ExternalInput").ap()
    do = lambda n, s: nc.dram_tensor(n, list(s), F32, kind="ExternalOutput").ap()
    x_all = di("x_all", (NT * 128, D))
    cst_d = di("cst", (128, NCST))
    cs_d = di("cs_tab", (128, 5, 2, NT * 128))
    w_gdn = di("w_gdn", (8, D, 512))
    w_ba = di("w_ba", (D, 16))
    w_ret = di("w_ret", (4, D, 1024))
    convw_d = di("conv_w", (128, 24, 4))
    small_d = di("small", (128, 128))
    gfin_d = di("gfin", (128, D))
    w_out = di("w_out", (D, D))
    w_mq = di("w_mq", (D, D))
    w_mk = di("w_mk", (D, D))
    w_mv = di("w_mv", (D, D))
    w_mo = di("w_mo", (D, D))
    w_gate = di("w_gate", (D, DFF))
    w_up = di("w_up", (D, DFF))
    w_down = di("w_down", (DFF, D))
    mem_d = di("mem", (256, D))
    ck_d = di("cache_k", (16, 256, D))
    cv_d = di("cache_v", (16, 256, D))
    sg_d = di("st_gdn", (16, 8, 128, 128))
    sr_d = di("st_ret", (16, 4, 256, 256))
    sc_d = di("st_conv", (128, 24, 16, 3))
    y_d = do("y", (TF, D))
    o_sgp = do("o_sgp", (8, 128, 128))
    o_cvp = do("o_cvp", (128, 24, 3))
    o_srp = do("o_srp", (4, 256, 256))
    o_mk = do("o_mk", (256, D))
    o_mv = do("o_mv", (256, D))
    o_sgs = do("o_sgs", (16, 8, 128, 128))
    o_cvs = do("o_cvs", (128, 24, 16, 3))
    o_srs = do("o_srs", (16, 4, 256, 256))
    dbg = {}
    if debug:
        for n, s in debug.items():
            if not n.startswith("_"):
                dbg[n] = do("dbg_" + n, s)

    def stop_at(tag):
        if debug and debug.get("_stop") == tag:
            raise _Stop()

    def dump(name, ap, keys):
        if not (debug and debug.get("_dump")):
            return
        d = nc.dram_tensor("dbg_" + name, list(ap.shape), ap.dtype, kind="ExternalOutput").ap()
        out_toks.append(S.dma("sp", d, ap, "st_dbg", reads=keys))

    es = ExitStack()
    sbt = lambda name, shape, dt: es.enter_context(nc.sbuf_tensor("sb_" + name, list(shape), dt))
    pst = lambda name, shape, dt: es.enter_context(nc.psum_tensor("pp_" + name, list(shape), dt))
    S = Sched(nc)
    S.bank_of.update({k_: "B_" + k_ for k_ in ["ps0", "ps1", "ps2", "ps3", "pT0", "pT1", "P_sc0", "P_sc1", "P_o", "ffn_pg"]})
    S.limit = debug.get("_maxops") if debug else None
    out_toks = []

    cst = sbt("cst", (128, NCST), F32)
    small = sbt("small", (128, 128), F32)
    identb = sbt("identb", (128, 128), BF16)
    onesb = sbt("onesb", (128, 128), BF16)
    hT = sbt("hT", (128, 16, TF), BF16)
    mixT = sbt("mixT", (128, 16, TF), BF16)
    XA = sbt("xarena", (128, 9 * D), F32)
    wbuf = [sbt(f"wbuf{i}", (128, 16, 512), BF16) for i in range(2)]
    WA = sbt("warena", (128, 6160), F32)
    ps = [pst(f"ps{i}", (128, 512), F32) for i in range(8)]

    def C(name):
        o, n = CST[name]
        return cst[:, o:o + n]

    epsc = small[:, 100:104]
    ident = C("ident")
    ones = C("ones")

    def act(out, in_, func, reads, writes, **kw):
        return S.op("act", lambda e: e.activation(out=out, in_=in_, func=func, **kw), reads, writes)

    def mm(out, lhsT, rhs, start, stop, reads, writes, inc=None):
        return S.op("pe", lambda e: e.matmul(out, lhsT=lhsT, rhs=rhs, start=start, stop=stop), reads, writes,
                    inc=stop if inc is None else inc)

    def tr(out, in_, idn, reads, writes, inc=True):
        return S.op("pe", lambda e: e.transpose(out=out, in_=in_, identity=idn), reads, writes, inc=inc)

    def tt(eng, out, in0, in1, op, reads, writes):
        return S.op(eng, lambda e: e.tensor_tensor(out=out, in0=in0, in1=in1, op=op), reads, writes)

    def ts(eng, out, in0, s1, s2, op0, op1, reads, writes):
        if op1 is None:
            return S.op(eng, lambda e: e.tensor_scalar(out=out, in0=in0, scalar1=s1, scalar2=None, op0=op0), reads, writes)
        return S.op(eng, lambda e: e.tensor_scalar(out=out, in0=in0, scalar1=s1, scalar2=s2, op0=op0, op1=op1), reads, writes)

    def stt(eng, out, in0, sc, in1, op0, op1, reads, writes):
        return S.op(eng, lambda e: e.scalar_tensor_tensor(out=out, in0=in0, scalar=sc, in1=in1, op0=op0, op1=op1), reads, writes)

    def cp(eng, out, in_, reads, writes):
        if eng == "act":
            return S.op("act", lambda e: e.copy(out=out, in_=in_), reads, writes)
        return S.op(eng, lambda e: e.tensor_copy(out=out, in_=in_), reads, writes)

    def silu(dst, src, scr, rkeys, skey, wkeys):
        act(scr, src, AF.Exp, rkeys, [skey], scale=-1.0)
        act(scr, scr, AF.Ln, [skey, "small"], [skey], bias=epsc[:, 3:4])
        act(scr, scr, AF.Exp, [skey], [skey], scale=-1.0)
        tt("dve", dst, src, scr, ALU.mult, list(rkeys) + [skey], wkeys)

    def memset(eng, ap, v, writes):
        return S.op(eng, lambda e: e.memset(ap, v), (), writes)

    S.dma("sp", cst[:], cst_d[:, :], "ld_c", writes=["cst"])
    S.dma("sp", small[:], small_d[:, :], "ld_c_sm", writes=["small"])
    cp("dve", identb[:], ident, ["cst"], ["identb"])
    cp("dve", onesb[:], ones, ["cst"], ["identb"])
    KC = ["cst", "identb", "small"]

    wa = Arena(WA, 6160)
    xt = wa.alloc((128, D))
    hb = wa.alloc((128, D), BF16)
    junk = wa.alloc((128, D), BF16)
    stat = wa.alloc((128, 8))
    pT = [ps[6].bitcast(BF16), ps[7].bitcast(BF16)]

    def norm_T(src_ap, src_keys, gcol, dstT, dcol, dkey, load_from=None):
        if load_from is not None:
            S.dma("sp", xt, load_from, "ld_x", writes=["xt"])
            src_ap, src_keys = xt, ["xt"]
        act(junk, src_ap, AF.Square, src_keys, ["stat0"], accum_out=stat[:, 0:1])
        act(stat[:, 1:2], stat[:, 0:1], AF.Ln, ["stat0"], ["stat1"], scale=1.0 / D, bias=epsc[:, 0:1])
        act(stat[:, 2:3], stat[:, 1:2], AF.Exp, ["stat1"], ["stat2"], scale=-0.5)
        ts("dve", hb, src_ap, stat[:, 2:3], None, ALU.mult, None, list(src_keys) + ["stat2"], ["hb"])
        for q in range(4):
            p = pT[q % 2]
            pk = f"pT{q % 2}"
            for j in range(4):
                kc = q * 4 + j
                tr(p[:, j * 128:(j + 1) * 128], hb[:, kc * 128:(kc + 1) * 128], identb[:], ["hb", "identb"], [pk], inc=(j == 3))
            g_b = gcol[:, q * 4:(q + 1) * 4].unsqueeze(2).to_broadcast([128, 4, 128])
            tt("dve", dstT[:, q * 4:(q + 1) * 4, dcol:dcol + 128], p[:, 0:512].rearrange("p (a b) -> p a b", a=4), g_b, ALU.mult,
               [pk, "small"], [dkey])

    wsem = [0]

    def load_w_half(dram_block, idx):
        i, hf = idx // 2, idx % 2
        flat = wbuf[i].rearrange("p a b -> p (a b)")[:, hf * 4096:(hf + 1) * 4096].rearrange("p (a b) -> p a b", a=16)
        S.dma("pool", flat, dram_block.rearrange("(kc p) n -> p kc n", p=128), f"ld_wh{idx}", writes=[f"wbuf{i}h{hf}", f"wbuf{i}"])
        return flat, f"wbuf{i}h{hf}"

    def load_w_half_n(dram_block, idx):
        i, hf = idx // 2, idx % 2
        flat = wbuf[i].rearrange("p a b -> p (a b)")[:, hf * 4096:hf * 4096 + 2048].rearrange("p (a b) -> p a b", a=16)
        S.dma("pool", flat, dram_block.rearrange("(kc p) n -> p kc n", p=128), f"ld_wh{idx}", writes=[f"wbuf{i}h{hf}", f"wbuf{i}"])
        return flat, f"wbuf{i}h{hf}"

    def load_w(dram_block, rows_kc, ncols, slot=None):
        if slot is None:
            i = wsem[0] % 2
            wsem[0] += 1
        else:
            i = slot
        buf = wbuf[i]
        flat = buf.rearrange("p a b -> p (a b)")[:, 0:rows_kc * ncols].rearrange("p (a b) -> p a b", a=rows_kc)
        S.dma("pool", flat, dram_block.rearrange("(kc p) n -> p kc n", p=128), f"ld_w{i}", writes=[f"wbuf{i}"])
        return flat, f"wbuf{i}"

    pacc = [0]

    def next_acc():
        i = pacc[0] % 2
        pacc[0] += 1
        return ps[i], f"ps{i}"

    def proj_fm(w_ap, wkey, fcs, srcT, skey, tokg, evac):
        for fc in fcs:
            for (t0, n) in tokg:
                p, pk = next_acc()
                for kc in range(16):
                    mm(p[:, 0:n], w_ap[:, kc, fc * 128:(fc + 1) * 128], srcT[:, kc, t0:t0 + n], kc == 0, kc == 15,
                       [wkey, skey], [pk])
                evac(fc, t0, n, p[:, 0:n], pk)

    xres = XA.rearrange("p (a b) -> p a b", a=9)

    def xkeys(i):
        return [("xres", i, cg) for cg in range(4)]

    def proj_res(w_dram, nk, srcT, skey, first=False):
        for cg in range(4):
            w_ap, wkey = load_w(w_dram[:, cg * 512:(cg + 1) * 512], nk, 512)
            for i in range(9):
                p, pk = next_acc()
                for kc in range(nk):
                    mm(p[:, :], srcT[:, kc, i * 128:(i + 1) * 128], w_ap[:, kc, :], kc == 0, kc == nk - 1, [wkey, skey], [pk])
                xs = xres[:, i, cg * 512:(cg + 1) * 512]
                if first:
                    S.dma("sp", xs, x_all[(NPRE + i) * 128:(NPRE + i + 1) * 128, cg * 512:(cg + 1) * 512], "ld_xr%d" % ((i * 4 + cg) % 6),
                          writes=[("xres", i, cg)])
                tt("dve", xs, xs, p[:, :], ALU.add, [pk, ("xres", i, cg)], [("xres", i, cg)])

    gmix = small[:, 32:48]
    gmem = small[:, 48:64]
    gcross = small[:, 64:80]
    gffn = small[:, 80:96]

    try:
        for i in range(NPRE):
            norm_T(None, None, gmix, mixT, i * 128, "mixT", load_from=x_all[i * 128:(i + 1) * 128, :])
        for i in range(9):
            norm_T(None, None, gmix, hT, i * 128, "hT", load_from=x_all[(NPRE + i) * 128:(NPRE + i + 1) * 128, :])

        dump("hT", hT[:, :, :], ["hT"])
        dump("hTp", mixT[:, :, 0:1024], ["mixT"])
        stop_at("A")
        S.barrier()
        xa = Arena(XA, 9 * D)
        wm = Arena(WA, 6160)
        u_m = wm.alloc((128, 16, 256), BF16)
        Sr = wm.alloc((128, 4, 512))
        Srb = wm.alloc((128, 4, 512), BF16)
        Sg = wm.alloc((128, 8, 128))
        ba = xa.alloc((128, NT, 16))
        gg = xa.alloc((128, NT, 8))
        beta = xa.alloc((128, NT, 8))
        nbeta = xa.alloc((128, NT, 8))
        hbeta = xa.alloc((128, NT, 8))
        Gtm = xa.alloc((128, NT, 8))
        nGtm = xa.alloc((128, NT, 8))
        bEG = xa.alloc((128, NT, 8))
        eGLG = xa.alloc((128, NT, 8))
        tmpB = xa.alloc((128, NT, 8))
        tmpB2 = xa.alloc((128, NT, 8))
        nexpA = xa.alloc((128, 8))
        gng = xa.alloc((128, 1))
        rng = xa.alloc((128, 8))
        cw = xa.alloc((128, 24, 4))
        ch = xa.alloc((128, 24, 3))
        wba = xa.alloc((128, 16, 16), BF16)
        Sgb = xa.alloc((128, 8, 128), BF16)
        class _Obj:
            pass

        mark = xa.off
        STR = []
        for sid in range(2):
            B = _Obj()
            B.sid = sid
            bk = [ps[2 + 3 * sid], ps[3 + 3 * sid], ps[4 + 3 * sid]]
            B.P_nrm = bk[0][:, 0:256]
            B.P_n2 = bk[0][:, 0:128]
            B.P_G = bk[0][:, 256:512]
            B.P_kq = bk[1][:, 0:256]
            B.P_PP = bk[1][:, 0:128]
            B.P_PPT = bk[1][:, 128:256]
            B.P_Y = bk[1][:, 256:512]
            B.P_U = bk[1][:, 256:384]
            B.P_Sg = bk[1][:, 384:512]
            B.P_O = bk[2][:, 0:256]
            bc16 = bk[2].bitcast(BF16)
            B.pTa = bc16[:, 512:768]
            B.pTb = bc16[:, 768:1024]
            B.acc = (ps[sid], f"ps{sid}")
            for nm_, bnk in (("P_nrm", 0), ("P_n2", 0), ("P_G", 0), ("P_kq", 1), ("P_PP", 1), ("P_PPT", 1), ("P_Y", 1), ("P_U", 1),
                             ("P_Sg", 1), ("P_O", 2), ("pTa0", 2), ("pTa1", 2), ("pTb0", 2), ("pTb1", 2)):
                S.bank_of[nm_ + f"_s{sid}"] = f"bank{2 + 3 * sid + bnk}"
            B.qkT = xa.alloc((128, 128), BF16)
            B.kd = xa.alloc((128, 128), BF16)
            B.qdT = xa.alloc((128, 128), BF16)
            B.sqo = xa.alloc((128, 128), BF16)
            B.rso = xa.alloc((128, 128))
            B.to = xa.alloc((128, 128))
            B.th = xa.alloc((128, 128))
            B.zz = xa.alloc((128, 128))
            B.pre = xa.alloc((128, 3, 259))
            B.preS = xa.alloc((128, 3, 16, 11))
            B.post = xa.alloc((128, 3, 256))
            B.thb = xa.alloc((128, 1, 256))
            B.zb = xa.alloc((128, 256))
            B.sq2 = xa.alloc((128, 256), BF16)
            B.rs2 = xa.alloc((128, 256))
            B.qkTb = xa.alloc((128, 256), BF16)
            B.vTb = xa.alloc((128, 128), BF16)
            B.gcum = xa.alloc((128, 256))
            _gb = B.gcum[:, 0:128].bitcast(BF16)
            B.QTb = [_gb[:, 0:128], _gb[:, 128:256]]
            B.t1 = xa.alloc((128, 128))
            B.t2 = xa.alloc((128, 128))
            B.E1 = xa.alloc((128, 128))
            B.E2 = xa.alloc((128, 128))
            B.Pb = [xa.alloc((128, 128), BF16) for _ in range(2)]
            B.PTb = [xa.alloc((128, 128), BF16) for _ in range(2)]
            B.Yb = [xa.alloc((128, 256), BF16) for _ in range(2)]
            B.wTb = xa.alloc((128, 128), BF16)
            B.ucT = xa.alloc((128, 128))
            B.eG2 = xa.alloc((128, 256))
            B.uTb = xa.alloc((128, 128), BF16)
            B.Ssm = xa.alloc((128, 8, 128))
            B.Ssmb = xa.alloc((128, 8, 128), BF16)
            B.u_m = u_m[:, :, 0:128] if sid == 0 else xa.alloc((128, 16, 128), BF16)
            STR.append(B)
        gdn_end = xa.off
        xa.off = mark
        qkT = xa.alloc((128, 128), BF16)
        kd = xa.alloc((128, 256), BF16)
        sqo = xa.alloc((128, 256), BF16)
        rso = xa.alloc((128, 128))
        to = xa.alloc((128, 256))
        th = xa.alloc((128, 256))
        zz = xa.alloc((128, 256))
        rb = xa.alloc((128, 4, 2, 256))
        csg = xa.alloc((128, 2, 2, 256))
        rt = [xa.alloc((128, 2, 256)) for _ in range(2)]
        rotb = xa.alloc((128, 2, 2, 256), BF16)
        vb = xa.alloc((128, 2, 128), BF16)
        Srs = [xa.alloc((128, 2, 256)) for _ in range(3)]
        Srsb = [xa.alloc((128, 2, 256), BF16) for _ in range(3)]
        xa.off = max(xa.off, gdn_end)
        P_kq = ps[3][:, 0:256]
        P_S = ps[4][:, :]
        P_O = ps[5][:, 0:256]
        P_n2 = ps[5][:, 384:512]
        pTa = pT[0]
        pTb = pT[1]
        S.bank_of.update({"P_kq": "bank3", "P_S": "bank4", "P_O": "bank5", "P_n2": "bank5",
                          "pTa0": "bank6", "pTb0": "bank7"})
        S.cap_shared = {"cst", "identb", "small", "hT", "mixT", "cw", "ch", "wba", "wbuf0", "wbuf1", "ps0", "ps1", "Sg", "Sgb",
                        "gg", "beta", "nbeta", "hbeta", "Gtm", "nGtm", "bEG", "eGLG", "gng", "rng"}

        S.dma("sp", cw, convw_d[:, :, :], "ld_c_cw", writes=["cw"])
        S.dma("pool", wba, w_ba.rearrange("(kc p) n -> p kc n", p=128), "ld_c2", writes=["wba"])
        memset("dve", ch, 0.0, ["ch"])
        memset("dve", Sg, 0.0, [("Sg", h) for h in range(8)])
        memset("dve", Sgb, 0.0, [("Sgb", h) for h in range(8)])
        memset("dve", Sr, 0.0, [("Sr", h) for h in range(4)])
        memset("dve", Srb, 0.0, [("Srb", h) for h in range(4)])
        memset("dve", u_m, 0.0, ["u_m"])

        for ti in range(NT):
            src, skey, c0 = (mixT, "mixT", ti * 128) if ti < NPRE else (hT, "hT", (ti - NPRE) * 128)
            pp = ps[2][:, ti * 16:(ti + 1) * 16]
            for kc in range(16):
                mm(pp, src[:, kc, c0:c0 + 128], wba[:, kc, :], kc == 0, kc == 15, [skey, "wba"], ["ps2"])
        cp("dve", ba.rearrange("p a b -> p (a b)"), ps[2][:, 0:NT * 16], ["ps2"], ["ba"])
        bv = ba[:, :, 0:8]
        av = ba[:, :, 8:16]
        act(tmpB, bv, AF.Exp, ["ba"], ["tmpB"], scale=-1.0)
        ts("dve", tmpB, tmpB, 1.0, None, ALU.add, None, ["tmpB"], ["tmpB"])
        S.op("dve", lambda e: e.reciprocal(out=beta, in_=tmpB), ["tmpB"], ["beta"])
        ts("dve", nbeta, beta, -1.0, None, ALU.mult, None, ["beta"], ["nbeta"])
        ts("dve", hbeta, beta, 0.5, None, ALU.mult, None, ["beta"], ["hbeta"])
        tt("dve", tmpB, av, small[:, 8:16].unsqueeze(1).to_broadcast([128, NT, 8]), ALU.add, ["ba", "small", "beta"], ["tmpB"])
        ts("dve", tmpB2, tmpB, -1.0, None, ALU.mult, None, ["tmpB"], ["tmpB2"])
        tt("dve", tmpB2, tmpB2, tmpB, ALU.max, ["tmpB", "tmpB2"], ["tmpB2"])
        act(tmpB2, tmpB2, AF.Exp, ["tmpB2"], ["tmpB2"], scale=-1.0)
        act(tmpB2, tmpB2, AF.Ln, ["tmpB2", "small"], ["tmpB2"], bias=epsc[:, 3:4])
        stt("dve", tmpB, tmpB, 0.0, tmpB2, ALU.max, ALU.add, ["tmpB", "tmpB2"], ["tmpB"])
        act(nexpA, small[:, 0:8], AF.Exp, ["small"], ["nexpA"])
        ts("dve", nexpA, nexpA, -1.0, None, ALU.mult, None, ["nexpA"], ["nexpA"])
        tt("dve", gg, tmpB, nexpA.unsqueeze(1).to_broadcast([128, NT, 8]), ALU.mult, ["tmpB", "nexpA"], ["gg"])
        ts("dve", gng, small[:, 16:17], 128.0 ** 0.5, None, ALU.mult, None, ["small"], ["gng"])
        ts("dve", rng, small[:, 17:25], 16.0, None, ALU.mult, None, ["small"], ["rng"])
        for ti in range(NT):
            kind = "S" if ti == NT - 1 else "P"
            m2 = C("mask2" + kind)
            mm(ps[3][:, ti * 8:(ti + 1) * 8], m2[:, 0:128], gg[:, ti, :], True, True, ["gg", "cst"], ["ps3"])
            mm(ps[3][:, 256 + ti * 8:256 + (ti + 1) * 8], m2[:, 128:256], gg[:, ti, :], True, True, ["gg", "cst"], ["ps3"])
        f2 = lambda a: a.rearrange("p a b -> p (a b)")
        cp("dve", f2(Gtm), ps[3][:, 0:NT * 8], ["ps3"], ["Gtm"])
        ts("dve", f2(nGtm), ps[3][:, 0:NT * 8], -1.0, None, ALU.mult, None, ["ps3"], ["nGtm"])
        act(f2(tmpB), f2(Gtm), AF.Exp, ["Gtm"], ["tmpB"])
        tt("dve", f2(bEG), f2(tmpB), f2(beta), ALU.mult, ["tmpB", "beta"], ["bEG"])
        tt("dve", f2(tmpB2), ps[3][:, 256:256 + NT * 8], f2(Gtm), ALU.subtract, ["ps3", "Gtm"], ["tmpB2"])
        act(f2(eGLG), f2(tmpB2), AF.Exp, ["tmpB2"], ["eGLG"])
        for nm_, ap_ in (("beta", beta), ("gg", gg), ("Gtm", Gtm), ("bEG", bEG), ("eGLG", eGLG)):
            dump(nm_, ap_, [nm_])
        stop_at("B")
        SCK = ["gg", "beta", "nbeta", "hbeta", "Gtm", "nGtm", "bEG", "eGLG", "gng", "rng"]

        def gdn_tile(B, h, ti, kind, qv, kv, vv, zv, so, mcol, inkeys):
            G_ = 128 // PCHUNK if kind == "P" else 16
            Cg = 128 // G_
            L = {64: 6, 128: 7}[PCHUNK] if kind == "P" else 3
            lo = 128 if so else 0
            col = lambda arr: arr[:, ti, h:h + 1]
            if not so:
                act(B.sq2[:, 0:128], qv, AF.Square, inkeys, ["sq2q"])
            act(B.sq2[:, 128:256], kv, AF.Square, inkeys, ["sq2k"])
            mm(B.P_nrm[:, lo:256], onesb[:], B.sq2[:, lo:256], True, True, ["sq2q", "sq2k", "identb"], ["P_nrm"])
            act(B.rs2[:, lo:256], B.P_nrm[:, lo:256], AF.Ln, ["P_nrm", "small"], ["rs2raw"], bias=epsc[:, 0:1])
            act(B.rs2[:, lo:256], B.rs2[:, lo:256], AF.Exp, ["rs2raw"], ["rs2"], scale=-0.5)
            if not so:
                stt("dve", B.qkTb[:, 0:128], qv, 128.0 ** -0.5, B.rs2[:, 0:128], ALU.mult, ALU.mult, inkeys + ["rs2"], ["qTb"])
            tt("dve", B.qkTb[:, 128:256], kv, B.rs2[:, 128:256], ALU.mult, inkeys + ["rs2"], ["kTb"])
            cp("act", B.vTb, vv, inkeys, ["vTb"])
            tr(B.pTa[:, 0:128], B.qkTb[:, 128:256], identb[:], ["kTb", "identb"], ["pTa0"])
            tr(B.pTa[:, 128:256], B.vTb, identb[:], ["vTb", "identb"], ["pTa1"])
            act(B.kd[:, 0:128], B.pTa[:, 0:128], AF.Copy, ["pTa0"] + SCK, ["kd"], scale=col(eGLG))
            act(B.Yb[0][:, 128:256], B.pTa[:, 0:128], AF.Copy, ["pTa0"] + SCK, ["Y0b"], scale=col(bEG))
            act(B.Yb[0][:, 0:128], B.pTa[:, 128:256], AF.Copy, ["pTa1"] + SCK, ["Y0a"], scale=col(beta))
            ts("dve", B.gcum, C("mask2" + kind), col(gg), None, ALU.mult, None, ["cst"] + SCK, ["gcum", "QT0", "QT1"])
            mm(B.P_G, ones, B.gcum, True, True, ["gcum", "cst"], ["P_G"])
            stt("dve", B.t1, B.P_G[:, 0:128], -1.0, C("negS" + kind), ALU.mult, ALU.add, ["P_G", "cst"], ["t1"])
            act(B.E1, B.t1, AF.Exp, ["t1"] + SCK, ["E1"], bias=col(Gtm))
            if not so:
                tt("dve", B.t2, B.P_G[:, 0:128], C("negC" + kind), ALU.add, ["P_G", "cst"], ["t2"])
                act(B.E2, B.t2, AF.Exp, ["t2"] + SCK, ["E2"], bias=col(nGtm))
            act(B.eG2, B.P_G, AF.Exp, ["P_G"], ["eG2"])
            mm(B.P_kq[:, lo:256], B.qkTb[:, 128:256], B.qkTb[:, lo:256], True, True, ["qTb", "kTb"], ["P_kq"])
            stt("dve", B.Pb[0], B.P_kq[:, 128:256], col(nbeta), B.E1, ALU.mult, ALU.mult, ["P_kq", "E1"] + SCK, ["P0"])
            if not so:
                tt("dve", B.qkT, B.P_kq[:, 0:128], B.E2, ALU.mult, ["P_kq", "E2"], ["qkT"])
                tt("dve", B.qdT[:, 0:128], B.qkTb[:, 0:128], B.eG2[:, 0:128], ALU.mult, ["qTb", "eG2"], ["qdT"])
            tr(B.pTb[:, 0:128], B.Pb[0], identb[:], ["P0", "identb"], ["pTb0"])
            cp("act", B.PTb[0], B.pTb[:, 0:128], ["pTb0"], ["PT0"])
            tt("dve", B.QTb[0], B.pTb[:, 0:128], identb[:], ALU.add, ["pTb0", "identb"], ["QT0", "gcum"])
            Yk = lambda i: [f"Y{i}a", f"Y{i}b"]
            for j in range(L):
                cur, nxt = j % 2, 1 - (j % 2)
                if j < L - 1:
                    mm(B.P_Y, B.QTb[cur], B.Yb[cur], True, True, Yk(cur) + [f"QT{cur}"], ["P_Y"])
                    cp("act", B.Yb[nxt], B.P_Y, ["P_Y"], Yk(nxt))
                    mm(B.P_PPT, B.Pb[cur], B.PTb[cur], True, True, [f"P{cur}", f"PT{cur}"], ["P_PPT"])
                    if j < L - 2:
                        cp("dve", B.PTb[nxt], B.P_PPT, ["P_PPT"], [f"PT{nxt}"])
                    tt("dve", B.QTb[nxt], B.P_PPT, ident, ALU.add, ["P_PPT", "cst"], [f"QT{nxt}", "gcum"])
                    if j < L - 2:
                        mm(B.P_PP, B.PTb[cur], B.Pb[cur], True, True, [f"P{cur}", f"PT{cur}"], ["P_PP"])
                        cp("dve", B.Pb[nxt], B.P_PP, ["P_PP"], [f"P{nxt}"])
                else:
                    for hf in range(2):
                        mm(B.P_Y[:, hf * 128:(hf + 1) * 128], B.Yb[cur][:, hf * 128:(hf + 1) * 128], B.QTb[cur], True, True,
                           Yk(cur) + [f"QT{cur}"], ["P_Y"], inc=(hf == 1))
                    cp("act", B.ucT, B.P_Y[:, 0:128], ["P_Y"], ["ucT"])
                    cp("dve", B.wTb, B.P_Y[:, 128:256], ["P_Y"], ["wTb"])
            rowm = C("row" + kind)
            chain = kind == "P"
            if chain:
                Sf = [Sg[:, h, :]] * G_
                Sb = [Sgb[:, h, :]] * G_
                skf = [("Sg", h)] * G_
                skb = [("Sgb", h)] * G_
            else:
                Sf = [B.Ssm[:, g % 8, :] for g in range(G_)]
                Sb = [B.Ssmb[:, g % 8, :] for g in range(G_)]
                skf = ["Ssm"] * G_
                skb = ["Ssmb"] * G_

            def part1(g):
                cols = slice(g * Cg, (g + 1) * Cg)
                mm(B.P_U[:, cols], Sb[g], B.wTb[:, cols], True, True, [skb[g], "wTb"], ["P_U"])
                tt("dve", B.uTb[:, cols], B.ucT[:, cols], B.P_U[:, cols], ALU.subtract, ["ucT", "P_U"], ["uTb"])

            def part2(g):
                cols = slice(g * Cg, (g + 1) * Cg)
                if not so:
                    mm(B.P_O[:, cols], Sb[g], B.qdT[:, cols], True, False, [skb[g], "qdT"], ["P_O"])
                    mm(B.P_O[:, cols], B.u_m[:, g, 0:128], B.qkT[:, cols], False, True, ["u_m", "qkT"], ["P_O"])
                mm(B.P_Sg, B.kd[:, 0:128], B.u_m[:, g, 0:128], True, True, ["kd", "u_m"], ["P_Sg"])
                stt("dve", Sf[g], Sf[g], B.eG2[:, 128 + g * Cg:129 + g * Cg], B.P_Sg, ALU.mult, ALU.add, [skf[g], "eG2", "P_Sg"], [skf[g]])
                cp("act", Sb[g], Sf[g], [skf[g]], [skb[g]])

            if chain:
                for g in range(G_):
                    part1(g)
                    tr(B.pTb[:, 128:256], B.uTb, identb[:], ["uTb", "identb"], ["pTb1"])
                    ts("dve", B.u_m[:, g, 0:128], B.pTb[:, 128:256], rowm[:, g:g + 1], None, ALU.mult, None, ["pTb1", "cst"], ["u_m"])
                    part2(g)
            else:
                for hf in range(2):
                    gs = range(hf * 8, hf * 8 + 8)
                    S.dma("sp", B.Ssm, sg_d[hf * 8:(hf + 1) * 8, h].rearrange("s k v -> k s v"), "ld_s", writes=["Ssm"])
                    cp("act", B.Ssmb, B.Ssm, ["Ssm"], ["Ssmb"])
                    for g in gs:
                        part1(g)
                    tr(B.pTb[:, 128:256], B.uTb, identb[:], ["uTb", "identb"], ["pTb1"])
                    tt("dve", B.u_m[:, hf * 8:(hf + 1) * 8, 0:128], B.pTb[:, 128:256].unsqueeze(1).to_broadcast([128, 8, 128]),
                       rowm[:, hf * 8:(hf + 1) * 8].unsqueeze(2).to_broadcast([128, 8, 128]), ALU.mult, ["pTb1", "cst"], ["u_m"])
                    for g in gs:
                        part2(g)
                    S.dma_out("sp", o_sgs[hf * 8:(hf + 1) * 8, h].rearrange("s k v -> k s v"), B.Ssm, "st_s", reads=["Ssm"])
            if so:
                return
            act(B.sqo[:, 0:128], B.P_O[:, 0:128], AF.Square, ["P_O"], ["sqo"])
            mm(B.P_n2, onesb[:], B.sqo[:, 0:128], True, True, ["sqo", "identb"], ["P_n2"])
            act(B.rso, B.P_n2, AF.Ln, ["P_n2", "small"], ["rsoraw"], bias=epsc[:, 1:2])
            act(B.rso, B.rso, AF.Exp, ["rsoraw"], ["rso"], scale=-0.5)
            tt("dve", B.to[:, 0:128], B.P_O[:, 0:128], B.rso, ALU.mult, ["P_O", "rso"], ["to"])
            silu(B.zz[:, 0:128], zv, B.th[:, 0:128], inkeys, "th", ["zz"])
            stt("dve", mixT[:, h, mcol:mcol + 128], B.to[:, 0:128], gng[:, 0:1], B.zz[:, 0:128], ALU.mult, ALU.mult,
                ["to", "zz", "gng"], ["mixT"])

        def gdn_head(B, h, pas):
            so = pas == "pre"
            if so:
                w_ap, wkey = load_w(w_gdn[h][:, 0:384], 16, 384, slot=B.sid)
                parts = [(1, 1), (2, 2)]
                srcT, skey = mixT, "mixT"
                groups = [("P", g_ * 256, 256, 2 * g_) for g_ in range(4)]
            else:
                w_ap, wkey = load_w(w_gdn[h], 16, 512, slot=B.sid)
                parts = [(0, 0), (1, 1), (2, 2), (3, 3)]
                srcT, skey = hT, "hT"
                groups = [("P", g_ * 256, 256, NPRE + 2 * g_) for g_ in range(4)] + [("S", 1024, 128, NPRE + 8)]
            for (kind, t0, n, ti0) in groups:
                cparts = [p for p, _ in parts if p < 3]
                if kind == "P":
                    for p in cparts:
                        cp("dve", B.pre[:, p, 0:3], ch[:, p * 8 + h, :], ["ch"], ["pre"])
                else:
                    for p in cparts:
                        S.dma("sp", B.preS[:, p, :, 0:3], sc_d[:, p * 8 + h, :, :], "ld_cvs", writes=["preS"])
                for (p, wc) in parts:
                    pp, pk = B.acc
                    for kc in range(16):
                        mm(pp[:, 0:n], w_ap[:, kc, wc * 128:(wc + 1) * 128], srcT[:, kc, t0:t0 + n], kc == 0, kc == 15, [wkey, skey], [pk])
                    if p == 3:
                        cp("act", B.zb[:, 0:n], pp[:, 0:n], [pk], ["zb"])
                    elif kind == "P":
                        cp("act", B.pre[:, p, 3:3 + n], pp[:, 0:n], [pk], ["pre"])
                    else:
                        cp("act", B.preS[:, p, :, 3:11], pp[:, 0:128].rearrange("p (s t) -> p s t", s=16), [pk], ["preS"])
                for p in cparts:
                    wcol = lambda i: cw[:, p * 8 + h, i:i + 1]
                    if kind == "P":
                        src = lambda i: B.pre[:, p, i:i + n]
                        dst = B.post[:, p, 0:n]
                        rk = ["pre", "cw"]
                    else:
                        src = lambda i: B.preS[:, p, :, i:i + 8]
                        dst = B.post[:, p, 0:128].rearrange("p (s t) -> p s t", s=16)
                        rk = ["preS", "cw"]
                    ts("dve", dst, src(0), wcol(0), None, ALU.mult, None, rk, [("post", p)])
                    for i in range(1, 4):
                        stt("dve", dst, src(i), wcol(i), dst, ALU.mult, ALU.add, rk + [("post", p)], [("post", p)])
                    silu(B.post[:, p, 0:n], B.post[:, p, 0:n], B.thb[:, 0, 0:n], [("post", p)], "thb", [("post", p)])
                    if kind == "P":
                        cp("dve", ch[:, p * 8 + h, :], B.pre[:, p, n:n + 3], ["pre"], ["ch"])
                    else:
                        S.dma_out("sp", o_cvs[:, p * 8 + h, :, :], B.preS[:, p, :, 8:11], "st_cv", reads=["preS"])
                inkeys = [("post", p) for p in cparts] + (["zb"] if not so else [])
                for j in range(n // 128):
                    c0 = j * 128
                    gdn_tile(B, h, ti0 + j, kind, B.post[:, 0, c0:c0 + 128], B.post[:, 1, c0:c0 + 128], B.post[:, 2, c0:c0 + 128],
                             B.zb[:, c0:c0 + 128], so, t0 + c0, inkeys)
            if so:
                pp, pk = B.acc
                for kc in range(16):
                    mm(pp[:, 0:3], w_ap[:, kc, 0:128], mixT[:, kc, 1021:1024], kc == 0, kc == 15, [wkey, "mixT"], [pk])
                cp("act", ch[:, h, :], pp[:, 0:3], [pk], ["ch"])
            if not so:
                S.dma_out("sp", o_sgp[h], Sg[:, h, :], "st_s2", reads=[("Sg", h)])

        def ret_tile(hr, kind, c0, so, mcol, Sf, Sb, skf, skb):
            G_ = 128 // PCHUNK if kind == "P" else 16
            Cg = 128 // G_
            gl = CST[f"rgl{kind}{hr}"]
            cols_t = slice(c0, c0 + 128)
            rowm = C("row" + kind)
            tr(pTa[:, 0:128], rotb[:, 1, 0, cols_t], identb[:], ["rotk", "identb"], ["pTa0"], inc=False)
            tr(pTa[:, 128:256], rotb[:, 1, 1, cols_t], identb[:], ["rotk", "identb"], ["pTa0"])
            act(kd, pTa[:, 0:256], AF.Copy, ["pTa0", "cst"], ["kd"], scale=C(f"rkd{kind}{hr}"))
            cp("act", vb, rb[:, 2, :, cols_t], ["rb2"], ["vb"])
            tr(pTb[:, 0:128], vb[:, 0, :], identb[:], ["vb", "identb"], ["pTb0"], inc=False)
            tr(pTb[:, 128:256], vb[:, 1, :], identb[:], ["vb", "identb"], ["pTb0"])
            if kind == "P":
                for g in range(G_):
                    ts("dve", u_m[:, g, :], pTb[:, 0:256], rowm[:, g:g + 1], None, ALU.mult, None, ["pTb0", "cst"], ["u_m"])
            else:
                tt("dve", u_m[:, :, :], pTb[:, 0:256].unsqueeze(1).to_broadcast([128, 16, 256]),
                   rowm[:, 0:16].unsqueeze(2).to_broadcast([128, 16, 256]), ALU.mult, ["pTb0", "cst"], ["u_m"])
            if not so:
                mm(P_kq[:, 0:128], rotb[:, 1, 0, cols_t], rotb[:, 0, 0, cols_t], True, False, ["rotk", "rotq"], ["P_kq"])
                mm(P_kq[:, 0:128], rotb[:, 1, 1, cols_t], rotb[:, 0, 1, cols_t], False, True, ["rotk", "rotq"], ["P_kq"])
                stt("dve", qkT, P_kq[:, 0:128], C(f"rcd{kind}{hr}"), C("caus" + kind), ALU.mult, ALU.mult, ["P_kq", "cst"], ["qkT"])
            for g in range(G_):
                cols = slice(g * Cg, (g + 1) * Cg)
                sf, sb, kf, kb = Sf(g), Sb(g), skf(g), skb(g)
                if not so:
                    for dvc in range(2):
                        dsl = slice(dvc * 128, (dvc + 1) * 128)
                        oc = slice(dvc * 128 + g * Cg, dvc * 128 + (g + 1) * Cg)
                        mm(P_O[:, oc], sb[:, 0, dsl], rotb[:, 0, 0, c0 + g * Cg:c0 + (g + 1) * Cg], True, False, [kb, "rotq"], ["P_O"])
                        mm(P_O[:, oc], sb[:, 1, dsl], rotb[:, 0, 1, c0 + g * Cg:c0 + (g + 1) * Cg], False, False, [kb, "rotq"], ["P_O"])
                        mm(P_O[:, oc], u_m[:, g, dsl], qkT[:, cols], False, True, ["u_m", "qkT"], ["P_O"])
                for dkc in range(2):
                    mm(P_S[:, dkc * 256:(dkc + 1) * 256], kd[:, dkc * 128:(dkc + 1) * 128], u_m[:, g, :], True, True, ["kd", "u_m"], ["P_S"],
                       inc=(dkc == 1))
                sff = sf.rearrange("p a b -> p (a b)")
                stt("dve", sff, sff, gl, P_S, ALU.mult, ALU.add, [kf, "P_S"], [kf])
                cp("act", sb.rearrange("p a b -> p (a b)"), sff, [kf], [kb])
                if kind == "S":
                    ret_sample_done(g)
            if so:
                return
            act(sqo, P_O, AF.Square, ["P_O"], ["sqo"])
            mm(P_n2, onesb[:], sqo[:, 0:128], True, False, ["sqo", "identb"], ["P_n2"])
            mm(P_n2, onesb[:], sqo[:, 128:256], False, True, ["sqo", "identb"], ["P_n2"])
            act(rso, P_n2, AF.Ln, ["P_n2", "small"], ["rsoraw"], bias=epsc[:, 2:3])
            act(rso, rso, AF.Exp, ["rsoraw"], ["rso"], scale=-0.5)
            tt("dve", to.rearrange("p (a b) -> p a b", a=2), P_O.rearrange("p (a b) -> p a b", a=2),
               rso.unsqueeze(1).to_broadcast([128, 2, 128]), ALU.mult, ["P_O", "rso"], ["to"])
            rgv = rb[:, 3, :, cols_t]
            silu(zz.rearrange("p (a b) -> p a b", a=2), rgv, th.rearrange("p (a b) -> p a b", a=2), ["rb3"], "th", ["zz"])
            for dvc in range(2):
                stt("dve", mixT[:, 8 + 2 * hr + dvc, mcol:mcol + 128], to[:, dvc * 128:(dvc + 1) * 128], rng[:, 2 * hr + dvc:2 * hr + dvc + 1],
                    zz[:, dvc * 128:(dvc + 1) * 128], ALU.mult, ALU.mult, ["to", "zz", "rng"], ["mixT"])

        ret_ctx = {}

        def ret_sample_done(g):
            hr = ret_ctx["hr"]
            i = g % 3
            out_toks.append(S.dma("sp", o_srs[g, hr].rearrange("(i two) v -> i two v", two=2), Srs[i], "st_rs%d" % i, reads=[("Srs", i)]))

        def ret_head(hr, pas):
            so = pas == "pre"
            ret_ctx["hr"] = hr
            if so:
                loads = [(w_ret[hr][:, 256:768], [(1, 0, 0), (1, 1, 1), (2, 0, 2), (2, 1, 3)])]
                srcT, skey = mixT, "mixT"
                groups = [("P", g * 256, 256, g * 256) for g in range(4)]
            else:
                loads = [(w_ret[hr][:, 0:512], [(0, 0, 0), (0, 1, 1), (1, 0, 2), (1, 1, 3)]),
                         (w_ret[hr][:, 512:1024], [(2, 0, 0), (2, 1, 1), (3, 0, 2), (3, 1, 3)])]
                srcT, skey = hT, "hT"
                groups = [("P", g * 256, 256, (NPRE * 128) + g * 256) for g in range(4)] + [("S", 1024, 128, (NPRE + 8) * 128)]
            wl = [load_w(blk, 16, 512) for blk, _ in loads]
            for (kind, t0, n, tabc) in groups:
                S.dma("sp", csg[:, 0, :, 0:n], cs_d[:, 1 + hr, :, tabc:tabc + n], "ld_cs", writes=["csg"])
                S.dma("sp", csg[:, 1, :, 0:n], cs_d[:, 0, :, tabc:tabc + n], "ld_cs", writes=["csg"])
                for (w_ap, wkey), (_, plist) in zip(wl, loads):
                    for (p, c, wc) in plist:
                        pp, pk = next_acc()
                        for kc in range(16):
                            mm(pp[:, 0:n], w_ap[:, kc, wc * 128:(wc + 1) * 128], srcT[:, kc, t0:t0 + n], kc == 0, kc == 15, [wkey, skey], [pk])
                        cp("act", rb[:, p, c, 0:n], pp[:, 0:n], [pk], [f"rb{p}"])
                pl = [1] if so else [0, 1]
                a, b = pl[0], pl[-1] + 1
                e = rb[:, a:b, 0, 0:n]
                o = rb[:, a:b, 1, 0:n]
                npq = b - a
                cosb = csg[:, a:b, 0, 0:n]
                sinb = csg[:, a:b, 1, 0:n]
                rkeys = [f"rb{p}" for p in pl] + ["csg"]
                tv = [rt[i][:, 0:npq, 0:n] for i in range(2)]
                wk = ["rotq", "rotk"] if not so else ["rotk"]
                tt("dve", tv[0], e, cosb, ALU.mult, rkeys, ["rt0"])
                tt("dve", tv[1], o, sinb, ALU.mult, rkeys, ["rt1"])
                tt("dve", rotb[:, a:b, 0, 0:n], tv[0], tv[1], ALU.subtract, ["rt0", "rt1"], wk)
                tt("dve", tv[0], o, cosb, ALU.mult, rkeys, ["rt0"])
                tt("dve", tv[1], e, sinb, ALU.mult, rkeys, ["rt1"])
                tt("dve", rotb[:, a:b, 1, 0:n], tv[0], tv[1], ALU.add, ["rt0", "rt1"], wk)
                for j in range(n // 128):
                    c0 = j * 128
                    if kind == "P":
                        ret_tile(hr, kind, c0, so, t0 + c0, lambda g: Sr[:, hr, :].rearrange("p (a b) -> p a b", a=2),
                                 lambda g: Srb[:, hr, :].rearrange("p (a b) -> p a b", a=2), lambda g: ("Sr", hr), lambda g: ("Srb", hr))
                    else:
                        ret_tile_sample(hr, c0, t0 + c0)
            if not so:
                out_toks.append(S.dma("sp", o_srp[hr].rearrange("(i two) v -> i two v", two=2),
                                      Sr[:, hr, :].rearrange("p (a b) -> p a b", a=2), "st_s3", reads=[("Sr", hr)]))

        def ret_tile_sample(hr, c0, mcol):
            loaded = set()

            def ensure(g):
                if g in loaded:
                    return
                loaded.add(g)
                i = g % 3
                S.dma("sp", Srs[i], sr_d[g, hr].rearrange("(i two) v -> i two v", two=2), "ld_rs%d" % i, writes=[("Srs", i)])
                cp("act", Srsb[i], Srs[i], [("Srs", i)], [("Srsb", i)])

            def Sf(g):
                ensure(g)
                return Srs[g % 3]

            def Sb(g):
                ensure(g)
                return Srsb[g % 3]

            ret_tile(hr, "S", c0, False, mcol, Sf, Sb, lambda g: ("Srs", g % 3), lambda g: ("Srsb", g % 3))

        if not (debug and debug.get("_nomixer")):
            def gdn_section(pas):
                S.barrier()
                for B in STR:
                    memset("dve", B.uTb, 0.0, ["x"])
                    memset("dve", B.pre, 0.0, ["x"])
                    memset("dve", B.u_m, 0.0, ["x"])
                S.barrier()
                for h0 in range(0, NH, 2):
                    caps = []
                    for sid in range(2):
                        if h0 + sid < NH:
                            S.capture, S.cap_sfx = [], f"_s{sid}"
                            gdn_head(STR[sid], h0 + sid, pas)
                            caps.append(S.capture)
                            S.capture = None
                    S.replay(caps)
                S.barrier()

            NH = debug.get("_nh", 8) if debug else 8
            NR = debug.get("_nr", 4) if debug else 4
            gdn_section("pre")
            dump("Sg_pre", Sg, [("Sg", h) for h in range(8)])
            dump("ch_pre", ch, ["ch"])
            stop_at("Gpre")
            for hr in range(NR):
                ret_head(hr, "pre")
            dump("Sr_pre", Sr, [("Sr", h) for h in range(4)])
            stop_at("Rpre")
            gdn_section("full")
            stop_at("Gfull")
            for hr in range(NR):
                ret_head(hr, "full")
            dump("mixT", mixT[:, :, :], ["mixT"])
            stop_at("Rfull")
            out_toks.append(S.dma("sp", o_cvp[:, :, :], ch, "st_cv2", reads=["ch"]))
        else:
            memset("dve", mixT[:], 0.0, ["mixT"])
        if "mixT" in dbg:
            S.barrier()
            for kc in range(16):
                cp("dve", XA[:, 0:TF], mixT[:, kc, :], ["mixT", "dbgx"], ["dbgx"])
                out_toks.append(S.dma("sp", dbg["mixT"][kc], XA[:, 0:TF], "st_dbg", reads=["dbgx"], writes=["dbgx"]))

        S.barrier()
        proj_res(w_out, 16, mixT, "mixT", first=True)

        KTraw = wa.alloc((128, D))
        KT = KTraw.bitcast(BF16).rearrange("p (a b) -> p a b", a=16)
        KTf = KTraw[:, 0:512]
        KTf2 = xt
        for i in range(9):
            norm_T(xres[:, i, :], xkeys(i), gcross, hT, i * 128, "hT")
        S.barrier()
        qT = mixT

        def evac_q(fc_global):
            def f(fc, t0, n, p, pk):
                cp("act", qT[:, fc_global(fc), t0:t0 + n], p, [pk], ["qT"])
            return f

        TOKG = [(0, 512), (512, 512), (1024, 128)]
        for cg in range(4):
            w_ap, wkey = load_w(w_mq[:, cg * 512:(cg + 1) * 512], 16, 512)
            proj_fm(w_ap, wkey, range(4), hT, "hT", TOKG, evac_q(lambda fc, cg=cg: cg * 4 + fc))
        S.barrier()
        ha = Arena(hT.rearrange("p a b -> p (a b)").bitcast(F32), 16 * TF // 2)
        mT = ha.alloc((128, 16, 256), BF16)
        Vb = ha.alloc((128, 2, D), BF16)
        Kb = ha.alloc((128, 2, D), BF16)
        kvo = ha.alloc((128, 512))
        sc = ha.alloc((128, 4, 256))
        pb = ha.alloc((128, 4, 256), BF16)
        pTs = ha.alloc((128, 8, 128), BF16)
        smx = ha.alloc((128, 16))
        oT = XA
        oTb = mT[:, :, 0:128]

        def mem_kv(src_rows_keys):
            pass

        for i in range(2):
            norm_T(None, None, gmem, mT, i * 128, "mT", load_from=mem_d[i * 128:(i + 1) * 128, :])
        for (wd, od, dst, dkey) in ((w_mk, o_mk, Kb, "Kb"), (w_mv, o_mv, Vb, "Vb")):
            for cg in range(4):
                w_ap, wkey = load_w(wd[:, cg * 512:(cg + 1) * 512], 16, 512)
                for i in range(2):
                    p, pk = next_acc()
                    for kc in range(16):
                        mm(p[:, :], mT[:, kc, i * 128:(i + 1) * 128], w_ap[:, kc, :], kc == 0, kc == 15, [wkey, "mT"], [pk])
                    cp("act", kvo, p[:, :], [pk], ["kvo"])
                    cp("dve", dst[:, i, cg * 512:(cg + 1) * 512], kvo, ["kvo"], [dkey])
                    out_toks.append(S.dma("sp", od[i * 128:(i + 1) * 128, cg * 512:(cg + 1) * 512], kvo, "st_kv", reads=["kvo"]))

        def make_KT(src, skey):
            for c4 in range(4):
                for i in range(2):
                    p = pT[(c4 * 2 + i) % 2]
                    pk = f"pT{(c4 * 2 + i) % 2}"
                    for j in range(4):
                        c = c4 * 4 + j
                        tr(p[:, j * 128:(j + 1) * 128], src[:, i, c * 128:(c + 1) * 128], identb[:], [skey, "identb"], [pk], inc=(j == 3))
                    cp("act", KT[:, c4 * 4:(c4 + 1) * 4, i * 128:(i + 1) * 128], p[:, 0:512].rearrange("p (a b) -> p a b", a=4), [pk], ["KT"])

        P_sc = [ps[2], ps[3]]
        P_o = ps[4][:, :]
        SCALE = 512.0 ** -0.5

        def attn_tile(i, seqs):
            tc_ = slice(i * 128, (i + 1) * 128)
            if seqs is None:
                for hh in range(4):
                    pp = P_sc[hh // 2][:, (hh % 2) * 256:(hh % 2 + 1) * 256]
                    for c in range(4):
                        mm(pp, qT[:, hh * 4 + c, tc_], KT[:, hh * 4 + c, :], c == 0, c == 3, ["qT", "KT"], [f"P_sc{hh // 2}"], inc=(c == 3))
                    cp("act", sc[:, hh, :], pp, [f"P_sc{hh // 2}"], ["sc"])
            softmax_and_out(i, tc_, None)

        def softmax_and_out(i, tc_, vsrc):
            S.op("dve", lambda e: e.reduce_max(out=smx[:, 0:4], in_=sc[:, :, :], axis=mybir.AxisListType.X), ["sc"], ["smx0"])
            ts("dve", smx[:, 4:8], smx[:, 0:4], -SCALE, None, ALU.mult, None, ["smx0"], ["smx1"])
            for hh in range(4):
                act(sc[:, hh, :], sc[:, hh, :], AF.Exp, ["sc", "smx1"], ["sc"], scale=SCALE, bias=smx[:, 4 + hh:5 + hh], accum_out=smx[:, 8 + hh:9 + hh])
            S.keys["smx2"] = [("act", S.count["act"]), []]
            S.op("dve", lambda e: e.reciprocal(out=smx[:, 12:16], in_=smx[:, 8:12]), ["smx2", "sc"], ["smx3"])
            tt("dve", pb, sc, smx[:, 12:16].unsqueeze(2).to_broadcast([128, 4, 256]), ALU.mult, ["sc", "smx3"], ["pb"])
            for hh in range(4):
                for mc in range(2):
                    tr(pTa[:, (hh * 2 + mc) * 128:(hh * 2 + mc + 1) * 128], pb[:, hh, mc * 128:(mc + 1) * 128], identb[:], ["pb", "identb"], ["pT0"],
                       inc=(hh == 3 and mc == 1))
            cp("act", pTs.rearrange("p a b -> p (a b)"), pTa[:, 0:1024], ["pT0"], ["pTs"])

        def attn_out_prompt(i):
            tc_ = slice(i * 128, (i + 1) * 128)
            for hh in range(4):
                for c in range(4):
                    oc = slice(c * 128, (c + 1) * 128)
                    for mc in range(2):
                        mm(P_o[:, oc], Vb[:, mc, (hh * 4 + c) * 128:(hh * 4 + c + 1) * 128], pTs[:, hh * 2 + mc, :], mc == 0, mc == 1, ["Vb", "pTs"], ["P_o"],
                           inc=(c == 3 and mc == 1))
                cp("act", oTb[:, hh * 4:(hh + 1) * 4, :], P_o.rearrange("p (a b) -> p a b", a=4), ["P_o"], ["oTb"])

        def apply_wo(i, wo_aps):
            for cg in range(4):
                w_ap, wkey = wo_aps[cg]
                p, pk = next_acc()
                for kc in range(16):
                    mm(p[:, :], oTb[:, kc, :], w_ap[:, kc, :], kc == 0, kc == 15, [wkey, "oTb"], [pk])
                xs = xres[:, i, cg * 512:(cg + 1) * 512]
                tt("dve", xs, xs, p[:, :], ALU.add, [pk, ("xres", i, cg)], [("xres", i, cg)])

        def stash_o(i):
            tc_ = slice(i * 128, (i + 1) * 128)
            cp("dve", qT[:, :, tc_], oTb[:, :, :], ["oTb", "qT"], ["qT"])

        S.barrier()
        make_KT(Kb, "Kb")
        for i in range(8):
            attn_tile(i, None)
            attn_out_prompt(i)
            stash_o(i)

        mrow = C("rowS")
        i = 8
        tc_ = slice(i * 128, (i + 1) * 128)
        memset("dve", sc, 0.0, ["sc"])
        oacc = ha.alloc((128, 16, 128), BF16) if False else oTb
        for s in range(16):
            kbuf, kkey = (Kb, "Kb") if s % 2 == 0 else (Vb, "Vb")
            S.dma("pool", kbuf, ck_d[s].rearrange("(c p) d -> p c d", p=128), "ld_ck%d" % (s % 2), writes=[kkey])
            make_KT(kbuf, kkey)
            for hh in range(4):
                pp = P_sc[hh // 2][:, (hh % 2) * 256:(hh % 2 + 1) * 256]
                for c in range(4):
                    mm(pp, qT[:, hh * 4 + c, tc_], KT[:, hh * 4 + c, :], c == 0, c == 3, ["qT", "KT"], [f"P_sc{hh // 2}"], inc=(c == 3))
                stt("dve", sc[:, hh, :], pp, mrow[:, s:s + 1], sc[:, hh, :], ALU.mult, ALU.add, [f"P_sc{hh // 2}", "sc", "cst"], ["sc"])
        softmax_and_out(i, tc_, None)
        for s in range(16):
            vbuf, vkey = (Vb, "Vb") if s % 2 == 0 else (Kb, "Kb")
            S.dma("pool", vbuf, cv_d[s].rearrange("(c p) d -> p c d", p=128), "ld_cv%d" % (s % 2), writes=[vkey])
            for hh in range(4):
                for c in range(4):
                    oc = slice(c * 128 + s * 8, c * 128 + s * 8 + 8)
                    for mc in range(2):
                        mm(P_o[:, oc], vbuf[:, mc, (hh * 4 + c) * 128:(hh * 4 + c + 1) * 128], pTs[:, hh * 2 + mc, s * 8:(s + 1) * 8], mc == 0, mc == 1,
                           [vkey, "pTs"], ["P_o"], inc=(c == 3 and mc == 1))
                cp("act", oTb[:, hh * 4:(hh + 1) * 4, s * 8:(s + 1) * 8], P_o.rearrange("p (a b) -> p a b", a=4)[:, :, s * 8:(s + 1) * 8], ["P_o"], ["oTb"])
        stash_o(i)
        S.barrier()
        proj_res(w_mo, 16, qT, "qT")

        S.barrier()
        for i in range(9):
            norm_T(xres[:, i, :], xkeys(i), gffn, hT, i * 128, "hT")
        actT = mixT
        gsb = wa.alloc((128, 512)) if False else None
        ffn_pair = [0]
        for fg in range(4):
            for j in range(11):
                fcg = fg * 11 + j
                if j % 2 == 0:
                    nfc = min(2, 11 - j)
                    pr_ = ffn_pair[0] % 2
                    ffn_pair[0] += 1
                    wg_ap, wgk = load_w_half(w_gate[:, fcg * 128:(fcg + 2) * 128], pr_ * 2) if nfc == 2 else load_w_half_n(w_gate[:, fcg * 128:(fcg + 1) * 128], pr_ * 2)
                    wu_ap, wuk = load_w_half(w_up[:, fcg * 128:(fcg + 2) * 128], pr_ * 2 + 1) if nfc == 2 else load_w_half_n(w_up[:, fcg * 128:(fcg + 1) * 128], pr_ * 2 + 1)
                jj = j % 2
                for (t0, n) in TOKG:
                    pg, pgk = ps[2], "ffn_pg"
                    pu, puk = next_acc()
                    for kc in range(16):
                        mm(pg[:, 0:n], wg_ap[:, kc, jj * 128:(jj + 1) * 128], hT[:, kc, t0:t0 + n], kc == 0, kc == 15, [wgk, wgk[:5], "hT"], [pgk])
                    for kc in range(16):
                        mm(pu[:, 0:n], wu_ap[:, kc, jj * 128:(jj + 1) * 128], hT[:, kc, t0:t0 + n], kc == 0, kc == 15, [wuk, wuk[:5], "hT"], [puk])
                    sg_ = kvo[:, 0:n] if False else None
                    act(KTf[:, 0:n], pg[:, 0:n], AF.Silu, [pgk], ["silu"])
                    tt("dve", actT[:, j, t0:t0 + n], KTf[:, 0:n], pu[:, 0:n], ALU.mult, ["silu", puk], ["actT"])
            for cg in range(4):
                w_ap, wkey = load_w(w_down[fg * 11 * 128:(fg + 1) * 11 * 128, cg * 512:(cg + 1) * 512], 11, 512)
                for i in range(9):
                    p, pk = next_acc()
                    for kc in range(11):
                        mm(p[:, :], actT[:, kc, i * 128:(i + 1) * 128], w_ap[:, kc, :], kc == 0, kc == 10, [wkey, "actT"], [pk])
                    xs = xres[:, i, cg * 512:(cg + 1) * 512]
                    tt("dve", xs, xs, p[:, :], ALU.add, [pk, ("xres", i, cg)], [("xres", i, cg)])

        S.barrier()
        gfin = KTf2
        S.dma("sp", gfin, gfin_d[:, :], "ld_c", writes=["gfin"])
        for i in range(9):
            act(junk, xres[:, i, :], AF.Square, xkeys(i), ["stat0"], accum_out=stat[:, 0:1])
            act(stat[:, 1:2], stat[:, 0:1], AF.Ln, ["stat0"], ["stat1"], scale=1.0 / D, bias=epsc[:, 0:1])
            act(stat[:, 2:3], stat[:, 1:2], AF.Exp, ["stat1"], ["stat2"], scale=-0.5)
            stt("dve", xres[:, i, :], xres[:, i, :], stat[:, 2:3], gfin, ALU.mult, ALU.mult, xkeys(i) + ["stat2", "gfin"], xkeys(i))
            out_toks.append(S.dma("sp", y_d[i * 128:(i + 1) * 128, :], xres[:, i, :], "st_y", reads=xkeys(i)))


    except _Stop:
        pass
    S.limit = None
    if debug:
        print("recorded ops", S.nrec, S.count, flush=True)
    S.wait_all("sp", out_toks + S.out_toks)
    S.emit()
    es.close()
    return nc


_NC_CACHE = {}


def _get_nc():
    if "nc" not in _NC_CACHE:
        _NC_CACHE["nc"] = build_program()
    return _NC_CACHE["nc"]


def _prep_inputs(inp):
    f = lambda k: np.ascontiguousarray(np.asarray(inp[k], dtype=np.float32))
    x_prompt, x_sample, mem_prompt = f("x_prompt"), f("x_sample"), f("mem_prompt")
    ck, cv = f("cache_mem_k")[0], f("cache_mem_v")[0]
    sg, scv, sr = f("state_gdn")[0], f("state_gdn_conv")[0], f("state_ret")[0]
    w_in = f("w_in")[0]
    i0, i1, i3, i4, i5, i6 = 3072, 4096, 4112, 5136, 6160, 7184
    w_gdn = np.stack([np.concatenate([w_in[:, h * 128:(h + 1) * 128], w_in[:, 1024 + h * 128:1024 + (h + 1) * 128],
                                      w_in[:, 2048 + h * 128:2048 + (h + 1) * 128], w_in[:, i0 + h * 128:i0 + (h + 1) * 128]], 1)
                      for h in range(8)])
    w_ba = np.ascontiguousarray(w_in[:, i1:i3])
    perm = np.concatenate([np.arange(0, 256, 2), np.arange(1, 256, 2)])
    w_ret = np.stack([np.concatenate([w_in[:, i3 + hr * 256 + perm], w_in[:, i4 + hr * 256 + perm],
                                      w_in[:, i5 + hr * 256:i5 + (hr + 1) * 256], w_in[:, i6 + hr * 256:i6 + (hr + 1) * 256]], 1)
                      for hr in range(4)])
    conv_w = np.ascontiguousarray(f("conv_w")[0].reshape(24, 128, 4).transpose(1, 0, 2))
    small = np.zeros((128, 128), np.float32)
    small[:, 0:8] = f("gdn_a_log")[0][None, :]
    small[:, 8:16] = f("gdn_dt_bias")[0][None, :]
    small[:, 16] = f("gdn_norm")[0]
    small[:, 17:25] = f("ret_norm")[0].reshape(8, 128).T
    small[:, 32:48] = f("norm_mix")[0].reshape(16, 128).T
    small[:, 48:64] = f("norm_mem_in")[0].reshape(16, 128).T
    small[:, 64:80] = f("norm_cross")[0].reshape(16, 128).T
    small[:, 80:96] = f("norm_ffn")[0].reshape(16, 128).T
    small[:, 100] = EPS
    small[:, 101] = 128 * EPS
    small[:, 102] = 256 * EPS
    small[:, 103] = 1.0
    gfin = np.ascontiguousarray(np.broadcast_to(f("norm_final")[None, :], (128, D)))
    shared = {"cst": CST_ARR, "w_gdn": w_gdn, "w_ba": w_ba, "w_ret": w_ret, "conv_w": conv_w, "small": small, "gfin": gfin,
              "w_out": f("w_out")[0], "w_mq": f("w_mem_q")[0], "w_mk": f("w_mem_k")[0], "w_mv": f("w_mem_v")[0], "w_mo": f("w_mem_o")[0],
              "w_gate": f("w_gate")[0], "w_up": f("w_up")[0], "w_down": f("w_down")[0]}
    maps = []
    for c in range(8):
        b, half = c // 2, c % 2
        pre = x_prompt[b, 0:1024] if half else np.zeros((1024, D), np.float32)
        x_all = np.concatenate([pre, x_prompt[b, half * 1024:(half + 1) * 1024], x_sample[16 * c:16 * c + 16].reshape(128, D)], 0)
        pos = np.concatenate([np.arange(1024), half * 1024 + np.arange(1024), 16384 + (np.arange(128) % 8)]).astype(np.float32)
        cos, sin = _rope_tables(pos)
        tabs = [np.stack([cos, sin], 1)]
        jj = np.concatenate([np.arange(2048) % PCHUNK, np.arange(128) % 8]).astype(np.float32)
        for hr in range(4):
            lg = np.log1p(-np.exp2(np.float32(-5.0 - hr))).astype(np.float32)
            eg = np.exp(lg * (jj + 1.0)).astype(np.float32)[None, :]
            tabs.append(np.stack([cos * eg, sin * eg], 1))
        m = dict(shared)
        m.update({
            "x_all": np.ascontiguousarray(x_all),
            "cs_tab": np.ascontiguousarray(np.stack(tabs, 1)),
            "mem": mem_prompt[b],
            "cache_k": np.ascontiguousarray(ck[16 * c:16 * c + 16].reshape(16, 256, D)),
            "cache_v": np.ascontiguousarray(cv[16 * c:16 * c + 16].reshape(16, 256, D)),
            "st_gdn": np.ascontiguousarray(sg[16 * c:16 * c + 16]),
            "st_ret": np.ascontiguousarray(sr[16 * c:16 * c + 16]),
            "st_conv": np.ascontiguousarray(scv[16 * c:16 * c + 16].transpose(2, 0, 1).reshape(24, 128, 16, 3).transpose(1, 0, 2, 3)),
        })
        maps.append(m)
    return maps


def _assemble(res):
    y_prompt = np.zeros((4, 2048, D), np.float32)
    y_sample = np.zeros((128, 8, D), np.float32)
    sgp = np.zeros((1, 4, 8, 128, 128), np.float32)
    cvp = np.zeros((1, 4, 3, 3072), np.float32)
    srp = np.zeros((1, 4, 4, 256, 256), np.float32)
    mkp = np.zeros((1, 4, 256, 4, 512), np.float32)
    mvp = np.zeros((1, 4, 256, 4, 512), np.float32)
    sgs = np.zeros((1, 128, 8, 128, 128), np.float32)
    cvs = np.zeros((1, 128, 3, 3072), np.float32)
    srs = np.zeros((1, 128, 4, 256, 256), np.float32)
    for c in range(8):
        r = res[c]
        b, half = c // 2, c % 2
        y = np.asarray(r["y"])
        y_prompt[b, half * 1024:(half + 1) * 1024] = y[0:1024]
        y_sample[16 * c:16 * c + 16] = y[1024:1152].reshape(16, 8, D)
        if half == 1:
            sgp[0, b] = np.asarray(r["o_sgp"])
            cvp[0, b] = np.asarray(r["o_cvp"]).transpose(1, 0, 2).reshape(3072, 3).T
            srp[0, b] = np.asarray(r["o_srp"])
        else:
            mkp[0, b] = np.asarray(r["o_mk"]).reshape(256, 4, 512)
            mvp[0, b] = np.asarray(r["o_mv"]).reshape(256, 4, 512)
        sgs[0, 16 * c:16 * c + 16] = np.asarray(r["o_sgs"])
        cvs[0, 16 * c:16 * c + 16] = np.asarray(r["o_cvs"]).transpose(2, 3, 1, 0).reshape(16, 3, 3072)
        srs[0, 16 * c:16 * c + 16] = np.asarray(r["o_srs"])
    return (y_prompt, y_sample, sgp, cvp, srp, mkp, mvp, sgs, cvs, srs)


def kernel(**inputs):
    nc = _get_nc()
    maps = _prep_inputs(inputs)
    r = run_bass_kernel_spmd(nc, maps, core_ids=list(range(8)))
    return _assemble(r.results)
```
